# Optimizing a Trainium2 kernel written in Bass

```python
import math
import jax, jax.numpy as jnp
from jax import lax
import numpy as np

D_MODEL = 2048
BATCH = 2
SEQ = 4096
DEPTH = 1

HEAD_DIM = 128
N_DIFF_HEADS = 8
N_NA_HEADS = 8
DIFF_HALF = HEAD_DIM // 2
DIFF_WIDTH = N_DIFF_HEADS * HEAD_DIM
NA_WIDTH = N_NA_HEADS * HEAD_DIM
MIX_WIDTH = DIFF_WIDTH + NA_WIDTH
IN_COLS = 3 * DIFF_WIDTH + 3 * NA_WIDTH
ROT_DIM = DIFF_HALF // 4
ROPE_THETA = 500000.0
GRID_W = 64
WIN_H = 8
WIN_W = 16
D_FF = 4 * D_MODEL
Q_BLOCK = 128
EPS = 1e-5
NEG_INF = -1e30

kernel_name = "hymba_diffattn_natten_sqrelu_encoder"


def rmsnorm(x, g):
    xf = x.astype(jnp.float32)
    y = xf * lax.rsqrt(jnp.mean(xf * xf, axis=-1, keepdims=True) + EPS)
    return (y * g.astype(jnp.float32)).astype(x.dtype)


def rotary_tables(seq, dtype):
    inv_freq = jnp.power(ROPE_THETA, -jnp.arange(0, ROT_DIM, 2, dtype=jnp.float32) / ROT_DIM)
    ang = jnp.arange(seq, dtype=jnp.float32)[:, None] * inv_freq[None, :]
    ang = jnp.concatenate([ang, ang], axis=-1)
    return jnp.cos(ang).astype(dtype), jnp.sin(ang).astype(dtype)


def partial_rotary(t, cos, sin):
    rot, rest = t[..., :ROT_DIM], t[..., ROT_DIM:]
    x1, x2 = rot[..., : ROT_DIM // 2], rot[..., ROT_DIM // 2:]
    rotated = jnp.concatenate([-x2, x1], axis=-1)
    c = cos[None, :, None, None, :]
    s = sin[None, :, None, None, :]
    return jnp.concatenate([rot * c + rotated * s, rest], axis=-1)


def diff_attention(q, k, v, lam, lambda_init, subln_g):
    B, S, H = q.shape[0], q.shape[1], q.shape[2]
    nqb = S // Q_BLOCK
    scale = 1.0 / math.sqrt(DIFF_HALF)
    k1, k2 = k[:, :, :, 0], k[:, :, :, 1]

    def to_blocks(t):
        return t.reshape(B, nqb, Q_BLOCK, H, DIFF_HALF).transpose(1, 0, 3, 2, 4)

    q1b, q2b = to_blocks(q[:, :, :, 0]), to_blocks(q[:, :, :, 1])

    def block(args):
        qa, qb = args
        s1 = jnp.einsum('bhqd,bkhd->bhqk', qa, k1).astype(jnp.float32) * scale
        s2 = jnp.einsum('bhqd,bkhd->bhqk', qb, k2).astype(jnp.float32) * scale
        a = jax.nn.softmax(s1, axis=-1) - lam * jax.nn.softmax(s2, axis=-1)
        return jnp.einsum('bhqk,bkhd->bqhd', a.astype(v.dtype), v)

    out = lax.map(block, (q1b, q2b))
    out = out.transpose(1, 0, 2, 3, 4).reshape(B, S, H, HEAD_DIM)
    out = rmsnorm(out, subln_g) * (1.0 - lambda_init)
    return out


def neighbourhood_attention(q, k, v, rel_bias):
    B, S, H, Dh = q.shape
    rows = S // GRID_W
    kh = min(WIN_H, rows)
    kw = WIN_W
    scale = 1.0 / math.sqrt(Dh)
    qg = q.reshape(B, rows, GRID_W, H, Dh)
    kg = k.reshape(B, rows, GRID_W, H, Dh)
    vg = v.reshape(B, rows, GRID_W, H, Dh)

    r = jnp.arange(rows)
    row_start = jnp.clip(r - kh // 2, 0, rows - kh)
    row_idx = row_start[:, None] + jnp.arange(kh)[None, :]
    c = jnp.arange(GRID_W)
    col_start = jnp.clip(c - kw // 2, 0, GRID_W - kw)
    kc = jnp.arange(GRID_W)
    col_in = (kc[None, :] >= col_start[:, None]) & (kc[None, :] < col_start[:, None] + kw)

    k_band = kg[:, row_idx]
    v_band = vg[:, row_idx]

    scores = jnp.einsum('brchd,brjkhd->bhrcjk', qg, k_band).astype(jnp.float32) * scale

    ri = row_idx - r[:, None] + (WIN_H - 1)
    ci = jnp.clip(kc[None, :] - c[:, None] + (WIN_W - 1), 0, 2 * WIN_W - 2)
    bias = rel_bias.astype(jnp.float32)[:, ri[:, None, :, None], ci[None, :, None, :]]

    logits = scores + bias[None]
    logits = jnp.where(col_in[None, None, None, :, None, :], logits, NEG_INF)
    probs = jax.nn.softmax(logits, axis=(-2, -1))
    out = jnp.einsum('bhrcjk,brjkhd->brchd', probs.astype(v.dtype), v_band)
    return out.reshape(B, S, H, Dh)


def setup_inputs(seed: int = 0) -> dict:
    key = jax.random.key(seed)
    ks = jax.random.split(key, 16)
    f32 = jnp.float32
    x = jax.random.normal(ks[0], (BATCH, SEQ, D_MODEL), f32)
    norm_mix_g = 1.0 + 0.02 * jax.random.normal(ks[1], (DEPTH, D_MODEL), f32)
    w_in = jax.random.normal(ks[2], (DEPTH, D_MODEL, IN_COLS), f32) * D_MODEL ** -0.5
    lambda_q1 = 0.1 * jax.random.normal(ks[3], (DEPTH, DIFF_HALF), f32)
    lambda_k1 = 0.1 * jax.random.normal(ks[4], (DEPTH, DIFF_HALF), f32)
    lambda_q2 = 0.1 * jax.random.normal(ks[5], (DEPTH, DIFF_HALF), f32)
    lambda_k2 = 0.1 * jax.random.normal(ks[6], (DEPTH, DIFF_HALF), f32)
    diff_subln_g = 1.0 + 0.02 * jax.random.normal(ks[7], (DEPTH, HEAD_DIM), f32)
    na_rel_bias = 0.02 * jax.random.normal(ks[8], (DEPTH, N_NA_HEADS, 2 * WIN_H - 1, 2 * WIN_W - 1), f32)
    w_out = jax.random.normal(ks[9], (DEPTH, MIX_WIDTH, D_MODEL), f32) * MIX_WIDTH ** -0.5
    norm_mlp_g = 1.0 + 0.02 * jax.random.normal(ks[10], (DEPTH, D_MODEL), f32)
    w_up = jax.random.normal(ks[11], (DEPTH, D_MODEL, D_FF), f32) * D_MODEL ** -0.5
    w_down = jax.random.normal(ks[12], (DEPTH, D_FF, D_MODEL), f32) * D_FF ** -0.5
    norm_final_g = 1.0 + 0.02 * jax.random.normal(ks[13], (D_MODEL,), f32)
    return {"x": x, "norm_mix_g": norm_mix_g, "w_in": w_in,
            "lambda_q1": lambda_q1, "lambda_k1": lambda_k1,
            "lambda_q2": lambda_q2, "lambda_k2": lambda_k2,
            "diff_subln_g": diff_subln_g, "na_rel_bias": na_rel_bias,
            "w_out": w_out, "norm_mlp_g": norm_mlp_g, "w_up": w_up,
            "w_down": w_down, "norm_final_g": norm_final_g}


def reference(x, norm_mix_g, w_in, lambda_q1, lambda_k1, lambda_q2, lambda_k2,
              diff_subln_g, na_rel_bias, w_out, norm_mlp_g, w_up, w_down, norm_final_g):
    B, S, _ = x.shape
    cos, sin = rotary_tables(S, x.dtype)
    for layer in range(DEPTH):
        lambda_init = 0.8 - 0.6 * math.exp(-0.3 * layer)
        h = rmsnorm(x, norm_mix_g[layer])
        proj = h @ w_in[layer]
        dq, dk, dv, nq, nk, nv = jnp.split(proj, 6, axis=-1)

        dq = partial_rotary(dq.reshape(B, S, N_DIFF_HEADS, 2, DIFF_HALF), cos, sin)
        dk = partial_rotary(dk.reshape(B, S, N_DIFF_HEADS, 2, DIFF_HALF), cos, sin)
        dv = dv.reshape(B, S, N_DIFF_HEADS, HEAD_DIM)
        lam = (jnp.exp(jnp.sum(lambda_q1[layer].astype(jnp.float32) * lambda_k1[layer].astype(jnp.float32)))
               - jnp.exp(jnp.sum(lambda_q2[layer].astype(jnp.float32) * lambda_k2[layer].astype(jnp.float32)))
               + lambda_init)
        a_out = diff_attention(dq, dk, dv, lam, lambda_init, diff_subln_g[layer])

        n_out = neighbourhood_attention(nq.reshape(B, S, N_NA_HEADS, HEAD_DIM),
                                        nk.reshape(B, S, N_NA_HEADS, HEAD_DIM),
                                        nv.reshape(B, S, N_NA_HEADS, HEAD_DIM),
                                        na_rel_bias[layer])

        mix = jnp.concatenate([a_out.reshape(B, S, DIFF_WIDTH), n_out.reshape(B, S, NA_WIDTH)], axis=-1)
        x = x + mix @ w_out[layer]

        u = rmsnorm(x, norm_mlp_g[layer]) @ w_up[layer]
        x = x + jnp.square(jax.nn.relu(u)) @ w_down[layer]
    return rmsnorm(x, norm_final_g)
```

```python
import math
import bisect
import numpy as np
import concourse.bass as bass
import concourse.mybir as mybir
from concourse.bass_utils import run_bass_kernel_spmd

F32 = mybir.dt.float32
BF16 = mybir.dt.bfloat16
AF = mybir.ActivationFunctionType
ALU = mybir.AluOpType
AX = mybir.AxisListType

D = 2048
SEQ = 4096
NOWN = 1024
DFF = 8192
EPS = 1e-5
NEG = -1e30


class Buf:
    __slots__ = ("name", "w", "r", "space", "lo", "hi", "dead", "excl")

    def __init__(self, name, space=None, lo=0, hi=0):
        self.name = name
        self.excl = False
        self.w = None
        self.r = []
        self.space = space
        self.lo = lo
        self.hi = hi
        self.dead = False


class Chan:
    def __init__(self, sem):
        self.sem = sem
        self.count = 0


class Prog:
    def __init__(self, nc):
        self.nc = nc
        self.ins = []
        self.bufs = []
        self.ctx = []
        self.eng_sem = {}
        for e in ("pe", "act", "dve", "pool"):
            cm = nc.semaphore("s_" + e)
            self.eng_sem[e] = cm.__enter__()
            self.ctx.append(cm)

    def chan(self, name):
        cm = self.nc.semaphore("c_" + name)
        s = cm.__enter__()
        self.ctx.append(cm)
        return Chan(s)

    def buf(self, name, space=None, lo=0, hi=0):
        b = Buf(name, space, lo, hi)
        if space is not None:
            for o in self.bufs:
                if o.space == space and not o.dead and o.lo < hi and lo < o.hi:
                    if o.w is not None:
                        b.r.append(o.w)
                    b.r.extend(o.r)
                    o.dead = True
            self.bufs = [o for o in self.bufs if not o.dead]
            self.bufs.append(b)
        return b

    def _add(self, eng, fn, reads, writes, chan=None):
        idx = len(self.ins)
        deps = set()
        for b in reads:
            assert not b.dead, b.name
            if b.w is not None:
                deps.add(b.w)
            if b.excl:
                for r_ in b.r:
                    if self.ins[r_]["eng"] != eng:
                        deps.add(r_)
        for b in writes:
            assert not b.dead, b.name
            if b.w is not None:
                deps.add(b.w)
            deps.update(b.r)
        for b in reads:
            b.r.append(idx)
        for b in writes:
            b.w = idx
            b.r = []
        tokval = None
        if chan is not None:
            chan.count += 1
            tokval = chan.count * 16
        self.ins.append(dict(eng=eng, fn=fn, deps=deps, chan=chan, tokval=tokval,
                             needed=False, cnt=None))
        return idx

    def op(self, eng, fn, reads=(), writes=()):
        return self._add(eng, fn, list(reads), list(writes))

    def dma(self, eng, fn, chan, reads=(), writes=()):
        return self._add(eng, fn, list(reads), list(writes), chan=chan)

    def wait_chan(self, eng, chan):
        self.ins.append(dict(eng=eng, fn=None, deps=set(), chan=None, tokval=None,
                             needed=False, cnt=None, waitfor=(chan.sem, chan.count * 16)))

    def emit(self):
        ins = self.ins
        chan_hist = {}
        for i, I in enumerate(ins):
            if I["chan"] is not None:
                chan_hist.setdefault(id(I["chan"]), []).append((i, I["tokval"]))
        for i, I in enumerate(ins):
            for d in I["deps"]:
                Dd = ins[d]
                if Dd["chan"] is None and not (Dd["eng"] == "pe" and I["eng"] == "pe"):
                    Dd["needed"] = True
        cnt = {e: 0 for e in self.eng_sem}
        for I in ins:
            if I["chan"] is None and I["needed"]:
                cnt[I["eng"]] += 1
                I["cnt"] = cnt[I["eng"]]
        per_eng = {e: [] for e in ("pe", "act", "dve", "pool", "sp")}
        for i, I in enumerate(ins):
            waits = {}
            if "waitfor" in I:
                s, v = I["waitfor"]
                waits[id(s)] = (s, v)
            for d in I["deps"]:
                Dd = ins[d]
                if Dd["chan"] is not None:
                    hist = chan_hist[id(Dd["chan"])]
                    k = bisect.bisect_left(hist, (i, 0)) - 1
                    s, v = Dd["chan"].sem, hist[k][1]
                else:
                    if Dd["eng"] == "pe" and I["eng"] == "pe":
                        continue
                    s, v = self.eng_sem[Dd["eng"]], Dd["cnt"]
                if id(s) not in waits or waits[id(s)][1] < v:
                    waits[id(s)] = (s, v)
            per_eng[I["eng"]].append((I, list(waits.values())))
        eng_sem = self.eng_sem

        def run(engname, e):
            known = {}
            for I, waits in per_eng[engname]:
                for s, v in waits:
                    if known.get(id(s), 0) < v:
                        e.wait_ge(s, v)
                        known[id(s)] = v
                if I["fn"] is None:
                    continue
                bi = I["fn"](e)
                if I["chan"] is not None:
                    bi.then_inc(I["chan"].sem, 16)
                elif I["needed"]:
                    bi.then_inc(eng_sem[engname], 1)

        with self.nc.Block() as block:
            @block.tensor
            def _(e):
                run("pe", e)

            @block.scalar
            def _(e):
                run("act", e)

            @block.vector
            def _(e):
                run("dve", e)

            @block.gpsimd
            def _(e):
                run("pool", e)

            @block.sync
            def _(e):
                run("sp", e)

    def close(self):
        for cm in reversed(self.ctx):
            cm.__exit__(None, None, None)


class Ring:
    def __init__(self, p, name, views, bufs, chans=True):
        self.views = views
        self.bufs = bufs
        self.chans = [p.chan(f"{name}{i}") for i in range(len(views))] if chans else None
        self.i = -1

    def next(self):
        self.i = (self.i + 1) % len(self.views)
        j = self.i
        return self.views[j], self.bufs[j], (self.chans[j] if self.chans else None)


ARENA_BYTES = 200 * 1024


def build_program(debug=False, upto=9):
    nc = bass.Bass("TRN2", target_bir_lowering=False)

    def din(name, shape, dt=F32):
        return nc.dram_tensor(name, list(shape), dt, kind="ExternalInput").ap()

    skind = "ExternalOutput" if debug else "Internal"

    def dscr(name, shape, dt=BF16):
        return nc.dram_tensor(name, list(shape), dt, kind=skind).ap()

    xp = din("xp", [SEQ, D])
    xh = din("xh", [512, D])
    cosx = din("cosx", [SEQ, 16])
    sinx = din("sinx", [SEQ, 16])
    tbl = din("tbl", [8, 2, 8, 128, 512])
    w_in = din("w_in", [D, 6144])
    w_out = din("w_out", [D, D])
    w_up = din("w_up", [D, DFF])
    w_down = din("w_down", [DFF, D])
    g_mix = din("g_mix", [D])
    g_mlp = din("g_mlp", [D])
    g_fin = din("g_fin", [D])
    lq1 = din("lq1", [64])
    lk1 = din("lk1", [64])
    lq2 = din("lq2", [64])
    lk2 = din("lk2", [64])
    subg = din("subg", [128])
    out = nc.dram_tensor("out", [NOWN, D], F32, kind="ExternalOutput").ap()

    KTd = dscr("KTd", [8, 128, SEQ])
    Vd = dscr("Vd", [SEQ, 1024])
    QTd = dscr("QTd", [8, 128, NOWN])
    KTn = dscr("KTn", [8, 128, 1536])
    Vn = dscr("Vn", [1536, 1024])
    QTn = dscr("QTn", [8, 128, NOWN])
    mixdbg = nc.dram_tensor("mixdbg", [128, 16, NOWN], BF16, kind="ExternalOutput").ap() if debug else None

    with (
        nc.sbuf_tensor("arena", [128, ARENA_BYTES // 4], F32) as arena,
        nc.psum_tensor("psum", [128, 4096], F32) as psum,
    ):
        p = Prog(nc)
        c_fin = p.chan("fin")

        def finish_early():
            p.dma("sp", lambda e: e.dma_start(out=out[0:128, 0:128], in_=identf), c_fin, reads=[b_identf], writes=[p.buf("outd")])
            p.wait_chan("sp", c_fin)
            for ch in dbg_chans:
                p.wait_chan("sp", ch)
            p.emit()
            p.close()

        dbg_chans = []

        def sb(lo, shape, dt):
            esz = 4 if dt == F32 else 2
            n = int(np.prod(shape[1:])) * esz
            assert lo % 4 == 0 and n % 4 == 0 and lo + n <= ARENA_BYTES, (lo, n)
            v = arena[:, lo // 4:(lo + n) // 4]
            if dt != F32:
                v = v.bitcast(dt)
            if len(shape) == 3:
                v = v.rearrange("p (a b) -> p a b", a=shape[1])
            elif len(shape) == 4:
                v = v.rearrange("p (a b c) -> p a b c", a=shape[1], b=shape[2])
            return v, lo + n

        def sbuf_buf(name, lo, hi):
            return p.buf(name, "sb", lo, hi)

        class Alloc:
            def __init__(self, lo):
                self.lo = lo

            def take(self, name, shape, dt):
                lo = self.lo
                v, hi = sb(lo, shape, dt)
                self.lo = hi
                return v, sbuf_buf(name, lo, hi)

            def ring(self, name, n, shape, dt, chans=True):
                vs, bs = [], []
                for i in range(n):
                    v, b = self.take(f"{name}{i}", shape, dt)
                    vs.append(v)
                    bs.append(b)
                return Ring(p, name, vs, bs, chans=chans)

        def bank(j):
            return psum[:, j * 512:(j + 1) * 512]

        def bank_bf(j):
            return psum[:, j * 512:(j + 1) * 512].bitcast(BF16)

        PS = [p.buf(f"ps{j}") for j in range(8)]
        for b_ in PS:
            b_.excl = True

        al = Alloc(0)
        ident, b_ident = al.take("ident", [128, 128], BF16)
        ones, b_ones = al.take("ones", [128, 128], BF16)
        identf, b_identf = al.take("identf", [128, 128], F32)
        onesf, b_onesf = al.take("onesf", [128, 128], F32)
        zerosb, b_zeros = al.take("zerosb", [128, 128], BF16)
        epsc, b_eps = al.take("eps", [128, 1], F32)
        lamw, b_lam = al.take("lamw", [128, 8], F32)
        sg08, b_sg = al.take("sg08", [128, 1], F32)
        ltmp, b_ltmp = al.take("ltmp", [128, 4, 64], F32)
        lprod, b_lprod = al.take("lprod", [128, 2, 64], F32)
        gbc, b_g = al.take("gbc", [128, D], F32)
        stat, b_stat_unused = al.take("stat", [128, 64], F32)
        WLO = al.lo
        wviews = []
        for i in range(2):
            v, _ = sb(WLO + i * 16384, [128, 16, 512], BF16)
            wviews.append(v)
        wbufs = [sbuf_buf(f"w{i}", WLO + i * 16384, WLO + (i + 1) * 16384) for i in range(2)]
        wchans = [p.chan(f"w{i}") for i in range(2)]
        BASE = WLO + 2 * 16384
        MIXLO = ARENA_BYTES - 32768
        mixT, _ = sb(MIXLO, [128, 16, NOWN], BF16)

        c_misc = p.chan("misc")
        c_g = p.chan("g")

        p.op("pool", lambda e: e.memset(identf, 0.0), writes=[b_identf])
        p.op("pool", lambda e: e.affine_select(out=identf, in_=identf, pattern=[[-1, 128]],
                                               compare_op=ALU.not_equal, fill=1.0, base=0,
                                               channel_multiplier=1),
             reads=[b_identf], writes=[b_identf])
        p.op("dve", lambda e: e.tensor_copy(out=ident, in_=identf), reads=[b_identf], writes=[b_ident])
        p.op("dve", lambda e: e.memset(ones, 1.0), writes=[b_ones])
        p.op("dve", lambda e: e.memset(onesf, 1.0), writes=[b_onesf])
        p.op("dve", lambda e: e.memset(zerosb, 0.0), writes=[b_zeros])
        p.op("dve", lambda e: e.memset(epsc, EPS), writes=[b_eps])
        for i, src in enumerate((lq1, lk1, lq2, lk2)):
            p.dma("sp", lambda e, i=i, src=src: e.dma_start(out=ltmp[:, i, :], in_=src.partition_broadcast(128)),
                  c_misc, writes=[b_ltmp])
        p.dma("sp", lambda e: e.dma_start(out=sg08, in_=subg.rearrange("(p o) -> p o", o=1)), c_misc, writes=[b_sg])
        p.dma("sp", lambda e: e.dma_start(out=gbc, in_=g_mix.partition_broadcast(128)), c_g, writes=[b_g])
        p.op("dve", lambda e: e.tensor_tensor(out=lprod[:, 0, :], in0=ltmp[:, 0, :], in1=ltmp[:, 1, :], op=ALU.mult),
             reads=[b_ltmp], writes=[b_lprod])
        p.op("dve", lambda e: e.tensor_tensor(out=lprod[:, 1, :], in0=ltmp[:, 2, :], in1=ltmp[:, 3, :], op=ALU.mult),
             reads=[b_ltmp, b_lprod], writes=[b_lprod])
        p.op("dve", lambda e: e.reduce_sum(out=lamw[:, 0:2], in_=lprod, axis=AX.X), reads=[b_lprod], writes=[b_lam])
        p.op("act", lambda e: e.activation(out=lamw[:, 2:4], in_=lamw[:, 0:2], func=AF.Exp), reads=[b_lam], writes=[b_lam])
        p.op("dve", lambda e: e.tensor_tensor(out=lamw[:, 4:5], in0=lamw[:, 3:4], in1=lamw[:, 2:3], op=ALU.subtract),
             reads=[b_lam], writes=[b_lam])
        p.op("dve", lambda e: e.tensor_scalar(out=lamw[:, 4:5], in0=lamw[:, 4:5], scalar1=-0.2, scalar2=None, op0=ALU.add),
             reads=[b_lam], writes=[b_lam])
        p.op("dve", lambda e: e.tensor_scalar(out=sg08, in0=sg08, scalar1=0.8, scalar2=None, op0=ALU.mult),
             reads=[b_sg], writes=[b_sg])
        neglam = lamw[:, 4:5]

        if upto == 0:
            finish_early()
            return nc
        wq = []

        def wsrc(wt, r0, c0):
            return wt[r0:r0 + 2048, c0:c0 + 512].rearrange("(k p) n -> p k n", p=128)

        class WStream:
            def __init__(self):
                self.issued = 0
                self.got = 0
                self.hold = 10 ** 9

            def _issue(self):
                i = self.issued
                if i >= len(wq):
                    return
                s = i % 2
                src = wq[i]
                p.dma("pool", lambda e, s=s, src=src: e.dma_start(out=wviews[s], in_=src), wchans[s], writes=[wbufs[s]])
                self.issued += 1

            def get(self):
                i = self.got
                while self.issued <= min(i + 1, len(wq) - 1, self.hold):
                    self._issue()
                self.got += 1
                return wviews[i % 2], wbufs[i % 2]

        colbase = {"dq": 0, "dk": 1024, "dv": 2048, "nq": 3072, "nk": 4096, "nv": 5120}
        _G_ALL = [("dq", 0), ("dq", 1), ("dk", 0), ("dk", 1), ("dv", 0), ("dv", 1),
                  ("nq", 0), ("nq", 1), ("nk", 0), ("nk", 1), ("nv", 0), ("nv", 1)]
        _G_KV = [("dk", 0), ("dk", 1), ("dv", 0), ("dv", 1)]
        _G_HALO = [("nk", 0), ("nk", 1), ("nv", 0), ("nv", 1)]
        for lst in (_G_ALL, [("dk", 0), ("dk", 1)]):
            for (ty, g) in lst:
                wq.append(wsrc(w_in, 0, colbase[ty] + g * 512))
        N_A_W = len(wq)
        for cg in range(4):
            wq.append(wsrc(w_out, 0, cg * 512))
        for qd in range(4):
            for fg in range(4):
                wq.append(wsrc(w_up, 0, qd * 2048 + fg * 512))
            for cg in range(4):
                wq.append(wsrc(w_down, qd * 2048, cg * 512))
        W = WStream()
        W.hold = N_A_W - 1

        al = Alloc(BASE)
        xring = al.ring("xt", 3, [128, D], F32)
        hring = al.ring("h", 2, [128, D], BF16, chans=False)
        HTLO = al.lo
        hTb = []
        for i in range(2):
            v, _ = sb(al.lo, [128, 16, 1024], BF16)
            hTb.append(v)
            al.lo += 16 * 1024 * 2
        bhT = [[p.buf(f"hT{i}_{t}") for t in range(8)] for i in range(2)]
        hTh, _ = sb(al.lo, [128, 16, 512], BF16)
        al.lo += 16 * 512 * 2
        bhTh = [p.buf(f"hTh_{t}") for t in range(4)]
        halo_ready = [False] * 4
        cs_cv, cs_sv, b_csc, b_css = [], [], [], []
        for i in range(2):
            v, b = al.take(f"cs_c{i}", [128, 8, 16], F32)
            cs_cv.append(v)
            b_csc.append(b)
            v, b = al.take(f"cs_s{i}", [128, 8, 16], F32)
            cs_sv.append(v)
            b_css.append(b)
        c_cs = [p.chan("cs0"), p.chan("cs1")]
        sring = al.ring("stg", 4, [128, 512], BF16)
        srot_bufs = [p.buf(f"stgrot{i}") for i in range(4)]
        rtring_a = al.ring("rta", 2, [128, 8, 16], F32, chans=False)
        rtring_b = al.ring("rtb", 2, [128, 8, 16], F32, chans=False)
        tring = al.ring("stT", 3, [128, 4, 128], BF16)
        Rv, Rb = [], []
        for i in range(2):
            v, b = al.take(f"R{i}", [128, 16, 512], BF16)
            Rv.append(v)
            Rb.append(b)
        c_R = p.chan("R")
        assert al.lo <= ARENA_BYTES, al.lo
        resident = {}

        bKTd = [p.buf(f"KTd{h}") for h in range(8)]
        bVd = [p.buf(f"Vd{h}") for h in range(8)]
        bQTd = [p.buf(f"QTd{h}") for h in range(8)]
        bKTn = [p.buf(f"KTn{h}") for h in range(8)]
        bVn = [p.buf(f"Vn{h}") for h in range(8)]
        bQTn = [p.buf(f"QTn{h}") for h in range(8)]

        stat_i = [0]

        stat_bufs = [p.buf(f"stat{i}") for i in range(16)]

        def rms_tile(src_ap_fn, src_reads, gb_buf, out_view, out_buf, jk):
            si = stat_i[0] % 16
            stat_i[0] += 1
            st = stat[:, si * 4:si * 4 + 4]
            bst = stat_bufs[si]
            xa = src_ap_fn()
            jv, jb = jk
            p.op("act", lambda e: e.activation(out=jv, in_=xa, func=AF.Square, scale=1.0 / math.sqrt(D),
                                               accum_out=st[:, 0:1]),
                 reads=src_reads, writes=[jb, bst])
            p.op("act", lambda e: e.activation(out=st[:, 1:2], in_=st[:, 0:1], func=AF.Sqrt, bias=epsc, scale=1.0),
                 reads=[bst, b_eps], writes=[bst])
            p.op("dve", lambda e: e.reciprocal(out=st[:, 2:3], in_=st[:, 1:2]), reads=[bst], writes=[bst])
            p.op("dve", lambda e: e.scalar_tensor_tensor(out=out_view, in0=xa, scalar=st[:, 2:3], in1=gbc,
                                                         op0=ALU.mult, op1=ALU.mult),
                 reads=src_reads + [bst, gb_buf], writes=[out_buf])

        gemm_bank = [0]
        tp_bank = [0]

        def transposes_to(h_view, h_buf, dst, dst_bufs, col0, banks):
            for half in range(2):
                bj = banks[half]
                pb = bank_bf(bj)
                for k in range(8):
                    kk = half * 8 + k
                    p.op("pe", lambda e, kk=kk, k=k, pb=pb: e.transpose(pb[:, k * 128:(k + 1) * 128],
                                                                      h_view[:, kk * 128:(kk + 1) * 128], ident),
                         reads=[h_buf, b_ident], writes=[PS[bj]])
                eng = "act" if half == 0 else "dve"
                dv = dst[:, half * 8:(half + 1) * 8, col0:col0 + 128]
                src = pb.rearrange("p (a b) -> p a b", a=8)
                if eng == "act":
                    p.op("act", lambda e, dv=dv, src=src: e.copy(out=dv, in_=src), reads=[PS[bj]], writes=dst_bufs)
                else:
                    p.op("dve", lambda e, dv=dv, src=src: e.tensor_copy(out=dv, in_=src), reads=[PS[bj]], writes=dst_bufs)

        G_ALL = [("dq", 0), ("dq", 1), ("dk", 0), ("dk", 1), ("dv", 0), ("dv", 1),
                 ("nq", 0), ("nq", 1), ("nk", 0), ("nk", 1), ("nv", 0), ("nv", 1)]
        G_KV = [("dv", 0), ("dv", 1), ("dk", 0), ("dk", 1)]
        G_HALO = [("nk", 0), ("nk", 1), ("nv", 0), ("nv", 1)]
        subphases = [dict(kind="own", ntile=8, groups=G_ALL, tb=0),
                     dict(kind="rest", ntile=8, groups=G_KV, tb=1024),
                     dict(kind="rest", ntile=8, groups=G_KV, tb=2048),
                     dict(kind="rest", ntile=8, groups=G_KV, tb=3072)]

        c_csS = [p.chan("css0"), p.chan("css1")]
        h_inflight = {}

        def hT_stage1(si, t):
            if si >= 0:
                sp_ = subphases[si]
                par = si % 2
                tb = sp_["tb"]
                if t == 0:
                    p.dma("sp", lambda e, tb=tb, par=par: e.dma_start(
                        out=cs_cv[par], in_=cosx[tb:tb + 1024, :].rearrange("(t p) j -> p t j", p=128)),
                        c_cs[par], writes=[b_csc[par]])
                    p.dma("sp", lambda e, tb=tb, par=par: e.dma_start(
                        out=cs_sv[par], in_=sinx[tb:tb + 1024, :].rearrange("(t p) j -> p t j", p=128)),
                        c_csS[par], writes=[b_css[par]])
                src = xp[tb + t * 128: tb + (t + 1) * 128, :]
            else:
                src = xh[t * 128:(t + 1) * 128, :]
            xv, xb, xc = xring.next()
            p.dma("sp", lambda e, xv=xv, src=src: e.dma_start(out=xv, in_=src), xc, writes=[xb])
            hv, hb, _ = hring.next()
            rms_tile(lambda xv=xv: xv, [xb], b_g, hv, hb, (hv, hb))
            h_inflight[(si, t)] = (hv, hb)

        def hT_stage2(si, t):
            hv, hb = h_inflight.pop((si, t))
            if si >= 0:
                par = si % 2
                transposes_to(hv, hb, hTb[par], [bhT[par][t]], t * 128, (6, 7))
            else:
                transposes_to(hv, hb, hTh, [bhTh[t]], t * 128, (6, 7))
                halo_ready[t] = True

        def hT_tile_ops(si, t):
            hT_stage1(si, t)
            hT_stage2(si, t)

        last_pe = [None]
        for si, sp_ in enumerate(subphases):
            par = si % 2
            hT = hTb[par]
            ntile = sp_["ntile"]
            tb = sp_["tb"]
            if si == 0:
                for t in range(ntile):
                    hT_tile_ops(0, t)
            nxt_tiles = [(si + 1, t) for t in range(subphases[si + 1]["ntile"])] if si + 1 < len(subphases) else []
            if si == 0:
                nxt_tiles = [(-1, t) for t in range(4)] + nxt_tiles
            n_gt = len(sp_["groups"]) * ntile + (16 if si == 0 else 0)
            interval = max(1, n_gt // (len(nxt_tiles) + 3)) if nxt_tiles else 1
            stage2_q = []

            def insertion_point(flush=False):
                if len(stage2_q) >= 2 or (flush and stage2_q):
                    hT_stage2(*stage2_q.pop(0))
                if nxt_tiles:
                    tl_ = nxt_tiles.pop(0)
                    hT_stage1(*tl_)
                    stage2_q.append(tl_)

            gt = 0
            pending = []

            def pop_pending(force=False):
                while pending and (force or len(pending) > 2):
                    f_ = pending.pop(0)
                    if f_ is not None:
                        f_()

            if si == 0:
                for g_ in range(2):
                    src_ = wsrc(w_in, 0, colbase["dv"] + g_ * 512)
                    p.dma("pool", lambda e, g_=g_, src_=src_: e.dma_start(out=Rv[g_], in_=src_), c_R, writes=[Rb[g_]])
                    resident[("dv", g_)] = (Rv[g_], Rb[g_])
            for (ty, g) in sp_["groups"]:
                if sp_["kind"] == "rest":
                    if (ty, g) not in resident:
                        resident[(ty, g)] = W.get()
                    wv, wb = resident[(ty, g)]
                else:
                    wv, wb = W.get()
                tile_list = list(range(ntile))
                if sp_["kind"] == "own" and ty in ("nk", "nv"):
                    tile_list = tile_list + [8, 9, 10, 11]
                for t in tile_list:
                    bj = gemm_bank[0] % 4
                    gemm_bank[0] += 1
                    pbank = bank(bj)
                    if t >= 8:
                        assert halo_ready[t - 8]
                        hsrc, hbuf, tl = hTh, bhTh[t - 8], t - 8
                    else:
                        hsrc, hbuf, tl = hT, bhT[par][t], t
                    for k in range(16):
                        last_pe[0] = p.op("pe", lambda e, k=k, tl=tl, wv=wv, pbank=pbank, hsrc=hsrc: e.matmul(
                            pbank, lhsT=hsrc[:, k, tl * 128:(tl + 1) * 128], rhs=wv[:, k, :],
                            start=(k == 0), stop=(k == 15)),
                            reads=[hbuf, wb], writes=[PS[bj]])
                    if t >= 8:
                        dtok = (t - 8) * 128 if t < 10 else 1280 + (t - 10) * 128
                    elif ty in ("nk", "nv"):
                        dtok = 256 + 128 * t
                    else:
                        dtok = tb + t * 128
                    sv, sbf, sc = sring.next()
                    srb = srot_bufs[sring.i]
                    heads = [g * 4 + hh for hh in range(4)]
                    part2 = None
                    if ty in ("dv", "nv"):
                        p.op("act", lambda e, sv=sv, pbank=pbank: e.copy(out=sv, in_=pbank), reads=[PS[bj]], writes=[sbf, srb])
                        dst = (Vd if ty == "dv" else Vn)[dtok:dtok + 128, g * 512:(g + 1) * 512]
                        dbufs = [(bVd if ty == "dv" else bVn)[h] for h in heads]
                        p.dma("sp", lambda e, dst=dst, sv=sv: e.dma_start(out=dst, in_=sv), sc,
                              reads=[sbf, srb], writes=dbufs)
                    else:
                        if ty in ("dq", "dk"):
                            sv3 = sv.rearrange("p (g d) -> p g d", g=8)
                            pb3 = pbank.rearrange("p (g d) -> p g d", g=8)
                            p.op("act", lambda e, sv3=sv3, pb3=pb3: e.copy(out=sv3[:, :, 16:64], in_=pb3[:, :, 16:64]),
                                 reads=[PS[bj]], writes=[sbf])
                            ra, rab, _ = rtring_a.next()
                            rb, rbb, _ = rtring_b.next()
                            cc = cs_cv[par][:, t, :].unsqueeze(1).broadcast_to([128, 8, 16])
                            ss_ = cs_sv[par][:, t, :].unsqueeze(1).broadcast_to([128, 8, 16])
                            p.op("dve", lambda e, ra=ra, pb3=pb3, cc=cc: e.tensor_tensor(out=ra, in0=pb3[:, :, 0:16], in1=cc, op=ALU.mult),
                                 reads=[PS[bj], b_csc[par]], writes=[rab])
                            p.op("dve", lambda e, rb=rb, pb3=pb3, ss_=ss_: e.tensor_tensor(out=rb[:, :, 0:8], in0=pb3[:, :, 8:16], in1=ss_[:, :, 0:8], op=ALU.mult),
                                 reads=[PS[bj], b_css[par]], writes=[rbb])
                            p.op("dve", lambda e, rb=rb, pb3=pb3, ss_=ss_: e.tensor_tensor(out=rb[:, :, 8:16], in0=pb3[:, :, 0:8], in1=ss_[:, :, 8:16], op=ALU.mult),
                                 reads=[PS[bj], b_css[par], rbb], writes=[rbb])
                            p.op("dve", lambda e, sv3=sv3, ra=ra, rb=rb: e.tensor_tensor(out=sv3[:, :, 0:16], in0=ra, in1=rb, op=ALU.add),
                                 reads=[rab, rbb], writes=[srb])
                        else:
                            p.op("act", lambda e, sv=sv, pbank=pbank: e.copy(out=sv, in_=pbank), reads=[PS[bj]], writes=[sbf, srb])
                        dT = {"dq": QTd, "dk": KTd, "nq": QTn, "nk": KTn}[ty]
                        dB = {"dq": bQTd, "dk": bKTd, "nq": bQTn, "nk": bKTn}[ty]

                        def part2(sv=sv, sbf=sbf, srb=srb, sc=sc, dT=dT, dB=dB, heads=heads, dtok=dtok):
                            tj = 4 + tp_bank[0] % 2
                            tp_bank[0] += 1
                            pb = bank_bf(tj)
                            for hh in range(4):
                                last_pe[0] = p.op("pe", lambda e, hh=hh, pb=pb: e.transpose(pb[:, hh * 128:(hh + 1) * 128],
                                                                                           sv[:, hh * 128:(hh + 1) * 128], ident),
                                                  reads=[sbf, srb, b_ident], writes=[PS[tj]])
                            tv, tbf, tc = tring.next()
                            p.op("dve", lambda e, tv=tv, pb=pb: e.tensor_copy(out=tv, in_=pb[:, 0:512].rearrange("p (a b) -> p a b", a=4)),
                                 reads=[PS[tj]], writes=[tbf])
                            dst = dT[heads[0]:heads[0] + 4, :, dtok:dtok + 128].rearrange("h p t -> p h t")
                            p.dma("sp", lambda e, dst=dst, tv=tv: e.dma_start(out=dst, in_=tv), tc,
                                  reads=[tbf], writes=[dB[h] for h in heads])

                    pending.append(part2)
                    pop_pending()
                    gt += 1
                    if (nxt_tiles or stage2_q) and gt % interval == 0:
                        insertion_point()
            pop_pending(force=True)
            while nxt_tiles or stage2_q:
                insertion_point(flush=True)

        hT_region = sbuf_buf("hT_region", HTLO, HTLO + 2 * 16 * 1024 * 2 + 16 * 512 * 2)
        hT_region.r.append(last_pe[0])
        W.hold = 10 ** 9
        b_mix = [[sbuf_buf(f"mix{h}_{q}", MIXLO + h * 2048 + q * 1024, MIXLO + h * 2048 + (q + 1) * 1024)
                  for q in range(2)] for h in range(16)]

        if upto == 1:
            for ch in sring.chans + tring.chans:
                dbg_chans.append(ch)
            finish_early()
            return nc
        al = Alloc(BASE)
        qring = al.ring("QT", 2, [128, NOWN], BF16)
        kring = al.ring("KT", 2, [128, SEQ], BF16)
        vring = al.ring("V", 2, [128, 32, 128], BF16)
        pring = al.ring("P", 4, [128, 1024], BF16, chans=False)
        p2bufs = [p.buf(f"P2_{i}") for i in range(4)]
        accring = al.ring("acc", 2, [128, 512], F32, chans=False)
        ep = {}
        for nm in ("r1", "a", "r2", "b", "o", "sd", "rs", "lg0", "lg1", "lg2"):
            ep[nm] = al.take("ep_" + nm, [128, 512], F32)
        sqv, b_sq = al.take("sq", [128, 512], BF16)
        tblring = al.ring("tbl", 6, [128, 512], F32)
        assert al.lo <= MIXLO, al.lo

        def load_head_d(h):
            qv, qb, qc_ = qring.next()
            kv, kb_, kc = kring.next()
            vv, vb, vc = vring.next()
            p.dma("sp", lambda e: e.dma_start(out=qv, in_=QTd[h]), qc_, reads=[bQTd[h]], writes=[qb])
            p.dma("sp", lambda e: e.dma_start(out=kv, in_=KTd[h]), kc, reads=[bKTd[h]], writes=[kb_])
            for j4 in range(4):
                p.dma("sp", lambda e, j4=j4: e.dma_start(
                    out=vv[:, j4 * 8:(j4 + 1) * 8, :],
                    in_=Vd[j4 * 1024:(j4 + 1) * 1024, h * 128:(h + 1) * 128].rearrange("(kb p) d -> p kb d", p=128)),
                    vc, reads=[bVd[h]], writes=[vb])
            return (qv, qb), (kv, kb_), (vv, vb)

        SC_D = 1.0 / math.sqrt(64.0)
        deferred_b = []
        nxt = load_head_d(0)
        for h in range(8):
            (qv, qb), (kv, kb_), (vv, vb) = nxt
            if h + 1 < 8:
                nxt = load_head_d(h + 1)
            for qc in range(2):
                qs = slice(qc * 512, (qc + 1) * 512)
                accv, accb, _ = accring.next()

                def s_step(kb, qs=qs, qv=qv, qb=qb, kv=kv, kb_=kb_):
                    par = kb % 2
                    b1, b2 = 2 * par, 2 * par + 1
                    p.op("pe", lambda e: e.matmul(bank(b1), lhsT=kv[0:64, kb * 128:(kb + 1) * 128], rhs=qv[0:64, qs],
                                                  start=True, stop=True), reads=[kb_, qb], writes=[PS[b1]])
                    p.op("pe", lambda e: e.matmul(bank(b2), lhsT=kv[64:128, kb * 128:(kb + 1) * 128], rhs=qv[64:128, qs],
                                                  start=True, stop=True), reads=[kb_, qb], writes=[PS[b2]])
                    pv, pb_, _ = pring.next()
                    pb2_ = p2bufs[pring.i]
                    p.op("act", lambda e: e.activation(out=pv[:, 0:512], in_=bank(b1), func=AF.Exp, scale=SC_D),
                         reads=[PS[b1]], writes=[pb_])
                    p.op("act", lambda e: e.activation(out=pv[:, 512:1024], in_=bank(b2), func=AF.Exp, scale=SC_D),
                         reads=[PS[b2]], writes=[pb2_])
                    return (pv, pb_, pb2_)

                def o_step(kb, P, vv=vv, vb=vb, accv=accv, accb=accb):
                    st, sp__ = (kb == 0), (kb == 31)
                    p.op("pe", lambda e: e.matmul(bank(4), lhsT=vv[:, kb, :], rhs=P[0][:, 0:512], start=st, stop=sp__),
                         reads=[vb, P[1]], writes=[PS[4]])
                    p.op("pe", lambda e: e.matmul(bank(6), lhsT=ones, rhs=P[0][:, 0:512], start=st, stop=sp__),
                         reads=[b_ones, P[1]], writes=[PS[6]])
                    p.op("pe", lambda e: e.matmul(bank(5), lhsT=vv[:, kb, :], rhs=P[0][:, 512:1024], start=st, stop=sp__),
                         reads=[vb, P[2]], writes=[PS[5]])
                    if kb == 0:
                        p.op("dve", lambda e: e.tensor_copy(out=accv, in_=P[0][:, 512:1024]), reads=[P[2]], writes=[accb])
                    else:
                        p.op("dve", lambda e: e.tensor_tensor(out=accv, in0=accv, in1=P[0][:, 512:1024], op=ALU.add),
                             reads=[P[2], accb], writes=[accb])

                prev = s_step(0)
                for kb in range(32):
                    cur = prev
                    if kb + 1 < 32:
                        prev = s_step(kb + 1)
                    o_step(kb, cur)
                    if kb == 2 and deferred_b:
                        deferred_b.pop(0)()
                r1, a_, r2, b_, o_, sd, rs = (ep[n] for n in ("r1", "a", "r2", "b", "o", "sd", "rs"))
                p.op("pe", lambda e, accv=accv: e.matmul(bank(7), lhsT=onesf, rhs=accv, start=True, stop=True),
                     reads=[b_onesf, accb], writes=[PS[7]])
                t1, t2 = ep["lg0"], ep["lg1"]
                p.op("act", lambda e: e.activation(out=t1[0], in_=bank(6), func=AF.Ln), reads=[PS[6]], writes=[t1[1]])
                p.op("act", lambda e: e.activation(out=r1[0], in_=t1[0], func=AF.Exp, scale=-1.0), reads=[t1[1]], writes=[r1[1]])
                p.op("act", lambda e: e.activation(out=t2[0], in_=bank(7), func=AF.Ln), reads=[PS[7]], writes=[t2[1]])
                p.op("act", lambda e: e.activation(out=r2[0], in_=t2[0], func=AF.Exp, scale=-1.0), reads=[t2[1]], writes=[r2[1]])
                p.op("dve", lambda e: e.tensor_tensor(out=a_[0], in0=bank(4), in1=r1[0], op=ALU.mult),
                     reads=[PS[4], r1[1]], writes=[a_[1]])
                p.op("dve", lambda e: e.tensor_tensor(out=b_[0], in0=bank(5), in1=r2[0], op=ALU.mult),
                     reads=[PS[5], r2[1]], writes=[b_[1]])
                p.op("dve", lambda e: e.scalar_tensor_tensor(out=o_[0], in0=b_[0], scalar=neglam, in1=a_[0],
                                                             op0=ALU.mult, op1=ALU.add),
                     reads=[b_[1], a_[1], b_lam], writes=[o_[1]])
                p.op("dve", lambda e: e.tensor_tensor(out=sqv, in0=o_[0], in1=o_[0], op=ALU.mult), reads=[o_[1]], writes=[b_sq])

                def ep_part_b(h=h, qs=qs, qc=qc, o_=o_, sd=sd, rs=rs):
                    p.op("pe", lambda e: e.matmul(bank(7), lhsT=ones, rhs=sqv, start=True, stop=True),
                         reads=[b_ones, b_sq], writes=[PS[7]])
                    p.op("act", lambda e: e.activation(out=sd[0], in_=bank(7), func=AF.Ln, bias=epsc, scale=1.0 / 128.0),
                         reads=[PS[7], b_eps], writes=[sd[1]])
                    p.op("act", lambda e: e.activation(out=rs[0], in_=sd[0], func=AF.Exp, scale=-0.5), reads=[sd[1]], writes=[rs[1]])
                    p.op("dve", lambda e: e.scalar_tensor_tensor(out=mixT[:, h, qs], in0=o_[0], scalar=sg08, in1=rs[0],
                                                                 op0=ALU.mult, op1=ALU.mult),
                         reads=[o_[1], rs[1], b_sg], writes=[b_mix[h][qc]])

                deferred_b.append(ep_part_b)
        while deferred_b:
            deferred_b.pop(0)()

        if upto == 2:
            c_dbg = p.chan("dbg")
            dbg_chans.append(c_dbg)
            p.dma("sp", lambda e: e.dma_start(out=mixdbg, in_=mixT), c_dbg,
                  reads=[b_mix[h][q] for h in range(8) for q in range(2)], writes=[p.buf("mixdbg")])
            finish_early()
            return nc
        SC_N = 1.0 / math.sqrt(128.0)

        def load_head_n(h):
            qv, qb, qc_ = qring.next()
            kv, kb_, kc = kring.next()
            vv, vb, vc = vring.next()
            p.dma("sp", lambda e: e.dma_start(out=qv, in_=QTn[h]), qc_, reads=[bQTn[h]], writes=[qb])
            p.dma("sp", lambda e: e.dma_start(out=kv[:, 0:1536], in_=KTn[h]), kc, reads=[bKTn[h]], writes=[kb_])
            p.dma("sp", lambda e: e.dma_start(out=vv[:, 0:12, :], in_=Vn[:, h * 128:(h + 1) * 128].rearrange("(kb p) d -> p kb d", p=128)),
                  vc, reads=[bVn[h]], writes=[vb])
            return (qv, qb), (kv, kb_), (vv, vb)

        lgs = [ep["lg0"], ep["lg1"], ep["lg2"], ep["sd"]]
        lgi = [0]
        heads_n = {}
        heads_n[0] = load_head_n(0)
        steps = [(h, c, kb) for h in range(8) for c in range(2) for kb in range(8)]
        DEPTH = 3

        def n_s_step(i):
            h, c, kb = steps[i]
            (qv, qb), (kv, kb_), (vv, vb) = heads_n[h]
            qs = slice(c * 512, (c + 1) * 512)
            bj = i % 4
            k0 = c * 512 + kb * 128
            tv, tbf, tc = tblring.next()
            p.dma("sp", lambda e: e.dma_start(out=tv, in_=tbl[h, c, kb]), tc, writes=[tbf])
            p.op("pe", lambda e: e.matmul(bank(bj), lhsT=kv[:, k0:k0 + 128], rhs=qv[:, qs], start=True, stop=True),
                 reads=[kb_, qb], writes=[PS[bj]])
            lg = lgs[lgi[0] % 4]
            lgi[0] += 1
            p.op("dve", lambda e: e.scalar_tensor_tensor(out=lg[0], in0=bank(bj), scalar=SC_N, in1=tv,
                                                         op0=ALU.mult, op1=ALU.add),
                 reads=[PS[bj], tbf], writes=[lg[1]])
            p1f, p1b, _ = pring.next()
            p1 = p1f[:, 0:512]
            p.op("act", lambda e: e.activation(out=p1, in_=lg[0], func=AF.Exp), reads=[lg[1]], writes=[p1b])
            return (p1, p1b)

        def n_o_step(i, P1):
            h, c, kb = steps[i]
            (qv, qb), (kv, kb_), (vv, vb) = heads_n[h]
            qs = slice(c * 512, (c + 1) * 512)
            gpar = (h * 2 + c) % 2
            bo, bl = (4, 6) if gpar == 0 else (5, 7)
            if c == 0 and kb == 0 and h + 1 < 8:
                heads_n[h + 1] = load_head_n(h + 1)
            st, sp__ = (kb == 0), (kb == 7)
            p.op("pe", lambda e: e.matmul(bank(bo), lhsT=vv[:, c * 4 + kb, :], rhs=P1[0], start=st, stop=sp__),
                 reads=[vb, P1[1]], writes=[PS[bo]])
            if kb == 7:
                p.op("pe", lambda e: e.matmul(bank(bl), lhsT=zerosb, rhs=P1[0], start=False, stop=False),
                     reads=[b_zeros, P1[1]], writes=[PS[bl]])
            p.op("pe", lambda e: e.matmul(bank(bl), lhsT=ones, rhs=P1[0], start=st, stop=sp__),
                 reads=[b_ones, P1[1]], writes=[PS[bl]])
            if kb < 7:
                p.op("pe", lambda e: e.matmul(bank(bl), lhsT=zerosb, rhs=P1[0], start=False, stop=False),
                     reads=[b_zeros, P1[1]], writes=[PS[bl]])
            if kb == 7:
                t1 = ep["r1"] if gpar == 0 else ep["r2"]
                rr = ep["a"] if gpar == 0 else ep["b"]
                p.op("act", lambda e: e.activation(out=t1[0], in_=bank(bl), func=AF.Ln), reads=[PS[bl]], writes=[t1[1]])
                p.op("act", lambda e: e.activation(out=rr[0], in_=t1[0], func=AF.Exp, scale=-1.0), reads=[t1[1]], writes=[rr[1]])
                p.op("dve", lambda e: e.tensor_tensor(out=mixT[:, 8 + h, qs], in0=bank(bo), in1=rr[0], op=ALU.mult),
                     reads=[PS[bo], rr[1]], writes=[b_mix[8 + h][c]])

        sres = {}
        for i in range(len(steps) + DEPTH):
            if i < len(steps):
                h_i = steps[i][0]
                if h_i not in heads_n:
                    heads_n[h_i] = load_head_n(h_i)
                sres[i] = n_s_step(i)
            j = i - DEPTH
            if j >= 0:
                n_o_step(j, sres.pop(j))


        if debug:
            c_dbg = p.chan("dbg")
            p.dma("sp", lambda e: e.dma_start(out=mixdbg, in_=mixT), c_dbg,
                  reads=[b_mix[h][q] for h in range(16) for q in range(2)], writes=[p.buf("mixdbg")])

        if upto == 3:
            dbg_chans.append(c_dbg)
            finish_early()
            return nc
        al = Alloc(BASE)
        x1, _ = sb(al.lo, [128, 8, D], F32)
        X1LO = al.lo
        al.lo += 8 * D * 4
        b_x1 = [[sbuf_buf(f"x1_{t}_{cg}", X1LO + t * 8192 + cg * 2048, X1LO + t * 8192 + (cg + 1) * 2048)
                 for cg in range(4)] for t in range(8)]
        c_x1 = p.chan("x1")
        xnT, _ = sb(al.lo, [128, 16, NOWN], BF16)
        XNLO = al.lo
        al.lo += 32768
        hcring = al.ring("hc", 2, [128, D], BF16, chans=False)
        rtring = al.ring("rt", 3, [128, 512], F32, chans=False)
        junk2, b_junk2 = al.take("junk2", [128, D], BF16)
        assert al.lo <= MIXLO, al.lo

        for t in range(8):
            p.dma("sp", lambda e, t=t: e.dma_start(out=x1[:, t, :], in_=xp[t * 128:(t + 1) * 128, :]), c_x1,
                  writes=b_x1[t])
        gb = 0
        p.dma("sp", lambda e: e.dma_start(out=gbc, in_=g_mlp.partition_broadcast(128)), c_g, writes=[b_g])
        b_xnT = [p.buf(f"xnT{t}") for t in range(8)]
        xn_region = sbuf_buf("xnT_region", XNLO, XNLO + 32768)
        for t in range(8):
            b_xnT[t].r.extend(xn_region.r)
        c2_q = []

        def c2_stage1(t):
            hv, hb, _ = hcring.next()
            rms_tile(lambda t=t: x1[:, t, :], b_x1[t], b_g, hv, hb, (hv, hb))
            c2_q.append((t, hv, hb))

        def c2_stage2():
            t, hv, hb = c2_q.pop(0)
            banks = (4, 5) if t % 2 == 0 else (6, 7)
            transposes_to(hv, hb, xnT, [b_xnT[t]], t * 128, banks)

        for cg in range(4):
            wv, wb = W.get()
            for t in range(8):
                bj = gb % 4
                gb += 1
                for k in range(16):
                    p.op("pe", lambda e, k=k, t=t, wv=wv, bj=bj: e.matmul(bank(bj), lhsT=mixT[:, k, t * 128:(t + 1) * 128],
                                                                       rhs=wv[:, k, :], start=(k == 0), stop=(k == 15)),
                         reads=[b_mix[k][t // 4], wb], writes=[PS[bj]])
                xs = x1[:, t, cg * 512:(cg + 1) * 512]
                p.op("dve", lambda e, xs=xs, bj=bj: e.tensor_tensor(out=xs, in0=bank(bj), in1=xs, op=ALU.add),
                     reads=[PS[bj], b_x1[t][cg]], writes=[b_x1[t][cg]])
                if cg == 3:
                    if len(c2_q) >= 2:
                        c2_stage2()
                    c2_stage1(t)
        while c2_q:
            c2_stage2()

        aT = mixT
        b_aT = [[p.buf(f"aT{k}_{tc}") for tc in range(2)] for k in range(16)]
        mix_region = sbuf_buf("aT_region", MIXLO, MIXLO + 32768)
        for k in range(16):
            for tc in range(2):
                b_aT[k][tc].r.extend(mix_region.r)
        for qd in range(4):
            for fg in range(4):
                wv, wb = W.get()
                for fb in range(4):
                    kk = fg * 4 + fb
                    for tc in range(2):
                        bj = gb % 4
                        gb += 1
                        for k in range(16):
                            p.op("pe", lambda e, k=k, wv=wv, fb=fb, tc=tc, bj=bj: e.matmul(
                                bank(bj), lhsT=wv[:, k, fb * 128:(fb + 1) * 128], rhs=xnT[:, k, tc * 512:(tc + 1) * 512],
                                start=(k == 0), stop=(k == 15)),
                                reads=[wb] + [b_xnT[t] for t in range(tc * 4, tc * 4 + 4)], writes=[PS[bj]])
                        rv, rb_, _ = rtring.next()
                        p.op("act", lambda e, rv=rv, bj=bj: e.activation(out=rv, in_=bank(bj), func=AF.Relu),
                             reads=[PS[bj]], writes=[rb_])
                        p.op("dve", lambda e, rv=rv, kk=kk, tc=tc: e.tensor_tensor(out=aT[:, kk, tc * 512:(tc + 1) * 512],
                                                                                in0=rv, in1=rv, op=ALU.mult),
                             reads=[rb_], writes=[b_aT[kk][tc]])
            if qd == 3:
                p.dma("sp", lambda e: e.dma_start(out=gbc, in_=g_fin.partition_broadcast(128)), c_g, writes=[b_g])
                xn_all = [x for bb in b_xnT for x in ([bb.w] if bb.w is not None else []) + bb.r]
                alo = Alloc(XNLO)
                ostring = alo.ring("ost", 2, [128, D], F32)
                for b in ostring.bufs:
                    b.r.extend(xn_all)
                c_out = p.chan("out")
            for cg in range(4):
                wv, wb = W.get()
                for t in range(8):
                    bj = gb % 4
                    gb += 1
                    for k in range(16):
                        p.op("pe", lambda e, k=k, t=t, wv=wv, bj=bj: e.matmul(bank(bj), lhsT=aT[:, k, t * 128:(t + 1) * 128],
                                                                           rhs=wv[:, k, :], start=(k == 0), stop=(k == 15)),
                             reads=[b_aT[k][t // 4], wb], writes=[PS[bj]])
                    xs = x1[:, t, cg * 512:(cg + 1) * 512]
                    p.op("dve", lambda e, xs=xs, bj=bj: e.tensor_tensor(out=xs, in0=bank(bj), in1=xs, op=ALU.add),
                         reads=[PS[bj], b_x1[t][cg]], writes=[b_x1[t][cg]])
                    if qd == 3 and cg == 3:
                        ov, ob, _ = ostring.next()
                        rms_tile(lambda t=t: x1[:, t, :], b_x1[t], b_g, ov, ob, (junk2, b_junk2))
                        p.dma("sp", lambda e, ov=ov, t=t: e.dma_start(out=out[t * 128:(t + 1) * 128, :], in_=ov), c_out,
                              reads=[ob], writes=[p.buf("outd")])

        p.wait_chan("sp", c_out)
        if debug:
            p.wait_chan("sp", c_dbg)
        p.emit()
        p.close()
    return nc


def _rot_tables(order):
    inv = np.power(np.float32(500000.0), -np.arange(0, 16, 2, dtype=np.float32) / np.float32(16)).astype(np.float32)
    t = order.astype(np.float32)
    ang = (t[:, None] * inv[None, :]).astype(np.float32)
    c = np.cos(ang).astype(np.float32)
    s = np.sin(ang).astype(np.float32)
    c16 = np.concatenate([c, c], axis=1)
    s16 = np.concatenate([-s, s], axis=1)
    return np.ascontiguousarray(c16), np.ascontiguousarray(s16)


def _na_table(rel_bias, r):
    tblc = np.full((8, 2, 8, 128, 512), NEG, dtype=np.float32)
    k = np.arange(128)
    q = np.arange(512)
    for c in range(2):
        qrow = 16 * r + 8 * c + q // 64
        qcol = q % 64
        rs = np.clip(qrow - 4, 0, 56)
        cst = np.clip(qcol - 8, 0, 48)
        for kb in range(8):
            krow = 16 * r - 4 + 8 * c + 2 * kb + k // 64
            kcol = k % 64
            valid = ((krow[:, None] >= 0) & (krow[:, None] < 64)
                     & (krow[:, None] >= rs[None, :]) & (krow[:, None] < rs[None, :] + 8)
                     & (kcol[:, None] >= cst[None, :]) & (kcol[:, None] < cst[None, :] + 16))
            ri = np.clip(krow[:, None] - qrow[None, :] + 7, 0, 14)
            ci = np.clip(kcol[:, None] - qcol[None, :] + 15, 0, 30)
            g = rel_bias[:, ri, ci]
            tblc[:, c, kb] = np.where(valid[None], g, np.float32(NEG))
    return tblc


_NC_CACHE = {}


def kernel(x, norm_mix_g, w_in, lambda_q1, lambda_k1, lambda_q2, lambda_k2, diff_subln_g,
           na_rel_bias, w_out, norm_mlp_g, w_up, w_down, norm_final_g, _debug=False, _upto=9, _cores=None):
    x = np.asarray(x, dtype=np.float32)
    f = lambda a: np.ascontiguousarray(np.asarray(a, dtype=np.float32))
    w_in0, w_out0, w_up0, w_down0 = f(w_in[0]), f(w_out[0]), f(w_up[0]), f(w_down[0])
    rel = f(na_rel_bias[0])
    in_maps = []
    cores = list(range(8)) if _cores is None else list(_cores)
    for c in cores:
        b, r = c // 4, c % 4
        own = np.arange(r * 1024, (r + 1) * 1024)
        rest = np.concatenate([np.arange(0, r * 1024), np.arange((r + 1) * 1024, SEQ)])
        order = np.concatenate([own, rest])
        xpc = np.ascontiguousarray(x[b][order])
        xhc = np.zeros((512, D), dtype=np.float32)
        for i, row in enumerate(list(range(16 * r - 4, 16 * r)) + list(range(16 * r + 16, 16 * r + 20))):
            if 0 <= row < 64:
                xhc[i * 64:(i + 1) * 64] = x[b, row * 64:(row + 1) * 64]
        cx, sx = _rot_tables(order)
        in_maps.append({
            "xp": xpc, "xh": xhc, "cosx": cx, "sinx": sx, "tbl": _na_table(rel, r),
            "w_in": w_in0, "w_out": w_out0, "w_up": w_up0, "w_down": w_down0,
            "g_mix": f(norm_mix_g[0]), "g_mlp": f(norm_mlp_g[0]), "g_fin": f(norm_final_g),
            "lq1": f(lambda_q1[0]), "lk1": f(lambda_k1[0]), "lq2": f(lambda_q2[0]), "lk2": f(lambda_k2[0]),
            "subg": f(diff_subln_g[0]),
        })
    key = (bool(_debug), _upto)
    if key not in _NC_CACHE:
        _NC_CACHE[key] = build_program(debug=_debug, upto=_upto)
    nc = _NC_CACHE[key]
    res = run_bass_kernel_spmd(nc, in_maps, core_ids=list(range(len(cores))))
    outp = np.zeros((2, SEQ, D), dtype=np.float32)
    for i, c in enumerate(cores):
        b, r = c // 4, c % 4
        outp[b, r * 1024:(r + 1) * 1024] = res.results[i]["out"]
    if _debug:
        return outp, res
    return outp
```

```python
import math
import bisect
import numpy as np
import concourse.bass as bass
import concourse.mybir as mybir
from concourse.bass_utils import run_bass_kernel_spmd

F32 = mybir.dt.float32
BF16 = mybir.dt.bfloat16
AF = mybir.ActivationFunctionType
ALU = mybir.AluOpType
AX = mybir.AxisListType

D = 2048
SEQ = 4096
NOWN = 1024
DFF = 8192
EPS = 1e-5
NEG = -1e30


class Buf:
    __slots__ = ("name", "w", "r", "space", "lo", "hi", "dead", "excl")

    def __init__(self, name, space=None, lo=0, hi=0):
        self.name = name
        self.excl = False
        self.w = None
        self.r = []
        self.space = space
        self.lo = lo
        self.hi = hi
        self.dead = False


class Chan:
    def __init__(self, sem):
        self.sem = sem
        self.count = 0


class Prog:
    def __init__(self, nc):
        self.nc = nc
        self.ins = []
        self.bufs = []
        self.ctx = []
        self.eng_sem = {}
        for e in ("pe", "act", "dve", "pool"):
            cm = nc.semaphore("s_" + e)
            self.eng_sem[e] = cm.__enter__()
            self.ctx.append(cm)

    def chan(self, name):
        cm = self.nc.semaphore("c_" + name)
        s = cm.__enter__()
        self.ctx.append(cm)
        return Chan(s)

    def buf(self, name, space=None, lo=0, hi=0):
        b = Buf(name, space, lo, hi)
        if space is not None:
            for o in self.bufs:
                if o.space == space and not o.dead and o.lo < hi and lo < o.hi:
                    if o.w is not None:
                        b.r.append(o.w)
                    b.r.extend(o.r)
                    o.dead = True
            self.bufs = [o for o in self.bufs if not o.dead]
            self.bufs.append(b)
        return b

    def _add(self, eng, fn, reads, writes, chan=None):
        idx = len(self.ins)
        deps = set()
        for b in reads:
            assert not b.dead, b.name
            if b.w is not None:
                deps.add(b.w)
            if b.excl:
                for r_ in b.r:
                    if self.ins[r_]["eng"] != eng:
                        deps.add(r_)
        for b in writes:
            assert not b.dead, b.name
            if b.w is not None:
                deps.add(b.w)
            deps.update(b.r)
        for b in reads:
            b.r.append(idx)
        for b in writes:
            b.w = idx
            b.r = []
        tokval = None
        if chan is not None:
            chan.count += 1
            tokval = chan.count * 16
        self.ins.append(dict(eng=eng, fn=fn, deps=deps, chan=chan, tokval=tokval,
                             needed=False, cnt=None))
        return idx

    def op(self, eng, fn, reads=(), writes=()):
        return self._add(eng, fn, list(reads), list(writes))

    def dma(self, eng, fn, chan, reads=(), writes=()):
        return self._add(eng, fn, list(reads), list(writes), chan=chan)

    def wait_chan(self, eng, chan):
        self.ins.append(dict(eng=eng, fn=None, deps=set(), chan=None, tokval=None,
                             needed=False, cnt=None, waitfor=(chan.sem, chan.count * 16)))

    def emit(self):
        ins = self.ins
        chan_hist = {}
        for i, I in enumerate(ins):
            if I["chan"] is not None:
                chan_hist.setdefault(id(I["chan"]), []).append((i, I["tokval"]))
        for i, I in enumerate(ins):
            for d in I["deps"]:
                Dd = ins[d]
                if Dd["chan"] is None and not (Dd["eng"] == "pe" and I["eng"] == "pe"):
                    Dd["needed"] = True
        cnt = {e: 0 for e in self.eng_sem}
        for I in ins:
            if I["chan"] is None and I["needed"]:
                cnt[I["eng"]] += 1
                I["cnt"] = cnt[I["eng"]]
        per_eng = {e: [] for e in ("pe", "act", "dve", "pool", "sp")}
        for i, I in enumerate(ins):
            waits = {}
            if "waitfor" in I:
                s, v = I["waitfor"]
                waits[id(s)] = (s, v)
            for d in I["deps"]:
                Dd = ins[d]
                if Dd["chan"] is not None:
                    hist = chan_hist[id(Dd["chan"])]
                    k = bisect.bisect_left(hist, (i, 0)) - 1
                    s, v = Dd["chan"].sem, hist[k][1]
                else:
                    if Dd["eng"] == "pe" and I["eng"] == "pe":
                        continue
                    s, v = self.eng_sem[Dd["eng"]], Dd["cnt"]
                if id(s) not in waits or waits[id(s)][1] < v:
                    waits[id(s)] = (s, v)
            per_eng[I["eng"]].append((I, list(waits.values())))
        eng_sem = self.eng_sem

        def run(engname, e):
            known = {}
            for I, waits in per_eng[engname]:
                for s, v in waits:
                    if known.get(id(s), 0) < v:
                        e.wait_ge(s, v)
                        known[id(s)] = v
                if I["fn"] is None:
                    continue
                bi = I["fn"](e)
                if I["chan"] is not None:
                    bi.then_inc(I["chan"].sem, 16)
                elif I["needed"]:
                    bi.then_inc(eng_sem[engname], 1)

        with self.nc.Block() as block:
            @block.tensor
            def _(e):
                run("pe", e)

            @block.scalar
            def _(e):
                run("act", e)

            @block.vector
            def _(e):
                run("dve", e)

            @block.gpsimd
            def _(e):
                run("pool", e)

            @block.sync
            def _(e):
                run("sp", e)

    def close(self):
        for cm in reversed(self.ctx):
            cm.__exit__(None, None, None)


class Ring:
    def __init__(self, p, name, views, bufs, chans=True):
        self.views = views
        self.bufs = bufs
        self.chans = [p.chan(f"{name}{i}") for i in range(len(views))] if chans else None
        self.i = -1

    def next(self):
        self.i = (self.i + 1) % len(self.views)
        j = self.i
        return self.views[j], self.bufs[j], (self.chans[j] if self.chans else None)


ARENA_BYTES = 200 * 1024


def build_program(debug=False, upto=9):
    nc = bass.Bass("TRN2", target_bir_lowering=False)

    def din(name, shape, dt=F32):
        return nc.dram_tensor(name, list(shape), dt, kind="ExternalInput").ap()

    skind = "ExternalOutput" if debug else "Internal"

    def dscr(name, shape, dt=BF16):
        return nc.dram_tensor(name, list(shape), dt, kind=skind).ap()

    xp = din("xp", [SEQ, D])
    xh = din("xh", [512, D])
    cosx = din("cosx", [SEQ, 16])
    sinx = din("sinx", [SEQ, 16])
    tbl = din("tbl", [8, 2, 8, 128, 512])
    w_in = din("w_in", [D, 6144])
    w_out = din("w_out", [D, D])
    w_up = din("w_up", [D, DFF])
    w_down = din("w_down", [DFF, D])
    g_mix = din("g_mix", [D])
    g_mlp = din("g_mlp", [D])
    g_fin = din("g_fin", [D])
    lq1 = din("lq1", [64])
    lk1 = din("lk1", [64])
    lq2 = din("lq2", [64])
    lk2 = din("lk2", [64])
    subg = din("subg", [128])
    out = nc.dram_tensor("out", [NOWN, D], F32, kind="ExternalOutput").ap()

    KTd = dscr("KTd", [8, 128, SEQ])
    Vd = dscr("Vd", [SEQ, 1024])
    QTd = dscr("QTd", [8, 128, NOWN])
    KTn = dscr("KTn", [8, 128, 1536])
    Vn = dscr("Vn", [1536, 1024])
    QTn = dscr("QTn", [8, 128, NOWN])
    mixdbg = nc.dram_tensor("mixdbg", [128, 16, NOWN], BF16, kind="ExternalOutput").ap() if debug else None

    with (
        nc.sbuf_tensor("arena", [128, ARENA_BYTES // 4], F32) as arena,
        nc.psum_tensor("psum", [128, 4096], F32) as psum,
    ):
        p = Prog(nc)
        c_fin = p.chan("fin")

        def finish_early():
            p.dma("sp", lambda e: e.dma_start(out=out[0:128, 0:128], in_=identf), c_fin, reads=[b_identf], writes=[p.buf("outd")])
            p.wait_chan("sp", c_fin)
            for ch in dbg_chans:
                p.wait_chan("sp", ch)
            p.emit()
            p.close()

        dbg_chans = []

        def sb(lo, shape, dt):
            esz = 4 if dt == F32 else 2
            n = int(np.prod(shape[1:])) * esz
            assert lo % 4 == 0 and n % 4 == 0 and lo + n <= ARENA_BYTES, (lo, n)
            v = arena[:, lo // 4:(lo + n) // 4]
            if dt != F32:
                v = v.bitcast(dt)
            if len(shape) == 3:
                v = v.rearrange("p (a b) -> p a b", a=shape[1])
            elif len(shape) == 4:
                v = v.rearrange("p (a b c) -> p a b c", a=shape[1], b=shape[2])
            return v, lo + n

        def sbuf_buf(name, lo, hi):
            return p.buf(name, "sb", lo, hi)

        class Alloc:
            def __init__(self, lo):
                self.lo = lo

            def take(self, name, shape, dt):
                lo = self.lo
                v, hi = sb(lo, shape, dt)
                self.lo = hi
                return v, sbuf_buf(name, lo, hi)

            def ring(self, name, n, shape, dt, chans=True):
                vs, bs = [], []
                for i in range(n):
                    v, b = self.take(f"{name}{i}", shape, dt)
                    vs.append(v)
                    bs.append(b)
                return Ring(p, name, vs, bs, chans=chans)

        def bank(j):
            return psum[:, j * 512:(j + 1) * 512]

        def bank_bf(j):
            return psum[:, j * 512:(j + 1) * 512].bitcast(BF16)

        PS = [p.buf(f"ps{j}") for j in range(8)]
        for b_ in PS:
            b_.excl = True

        al = Alloc(0)
        ident, b_ident = al.take("ident", [128, 128], BF16)
        ones, b_ones = al.take("ones", [128, 128], BF16)
        identf, b_identf = al.take("identf", [128, 128], F32)
        onesf, b_onesf = al.take("onesf", [128, 128], F32)
        zerosb, b_zeros = al.take("zerosb", [128, 128], BF16)
        epsc, b_eps = al.take("eps", [128, 1], F32)
        lamw, b_lam = al.take("lamw", [128, 8], F32)
        sg08, b_sg = al.take("sg08", [128, 1], F32)
        ltmp, b_ltmp = al.take("ltmp", [128, 4, 64], F32)
        lprod, b_lprod = al.take("lprod", [128, 2, 64], F32)
        gbc, b_g = al.take("gbc", [128, D], F32)
        stat, b_stat_unused = al.take("stat", [128, 64], F32)
        WLO = al.lo
        wviews = []
        for i in range(2):
            v, _ = sb(WLO + i * 16384, [128, 16, 512], BF16)
            wviews.append(v)
        wbufs = [sbuf_buf(f"w{i}", WLO + i * 16384, WLO + (i + 1) * 16384) for i in range(2)]
        wchans = [p.chan(f"w{i}") for i in range(2)]
        BASE = WLO + 2 * 16384
        MIXLO = ARENA_BYTES - 32768
        mixT, _ = sb(MIXLO, [128, 16, NOWN], BF16)

        c_misc = p.chan("misc")
        c_g = p.chan("g")

        p.op("pool", lambda e: e.memset(identf, 0.0), writes=[b_identf])
        p.op("pool", lambda e: e.affine_select(out=identf, in_=identf, pattern=[[-1, 128]],
                                               compare_op=ALU.not_equal, fill=1.0, base=0,
                                               channel_multiplier=1),
             reads=[b_identf], writes=[b_identf])
        p.op("dve", lambda e: e.tensor_copy(out=ident, in_=identf), reads=[b_identf], writes=[b_ident])
        p.op("dve", lambda e: e.memset(ones, 1.0), writes=[b_ones])
        p.op("dve", lambda e: e.memset(onesf, 1.0), writes=[b_onesf])
        p.op("dve", lambda e: e.memset(zerosb, 0.0), writes=[b_zeros])
        p.op("dve", lambda e: e.memset(epsc, EPS), writes=[b_eps])
        for i, src in enumerate((lq1, lk1, lq2, lk2)):
            p.dma("sp", lambda e, i=i, src=src: e.dma_start(out=ltmp[:, i, :], in_=src.partition_broadcast(128)),
                  c_misc, writes=[b_ltmp])
        p.dma("sp", lambda e: e.dma_start(out=sg08, in_=subg.rearrange("(p o) -> p o", o=1)), c_misc, writes=[b_sg])
        p.dma("sp", lambda e: e.dma_start(out=gbc, in_=g_mix.partition_broadcast(128)), c_g, writes=[b_g])
        p.op("dve", lambda e: e.tensor_tensor(out=lprod[:, 0, :], in0=ltmp[:, 0, :], in1=ltmp[:, 1, :], op=ALU.mult),
             reads=[b_ltmp], writes=[b_lprod])
        p.op("dve", lambda e: e.tensor_tensor(out=lprod[:, 1, :], in0=ltmp[:, 2, :], in1=ltmp[:, 3, :], op=ALU.mult),
             reads=[b_ltmp, b_lprod], writes=[b_lprod])
        p.op("dve", lambda e: e.reduce_sum(out=lamw[:, 0:2], in_=lprod, axis=AX.X), reads=[b_lprod], writes=[b_lam])
        p.op("act", lambda e: e.activation(out=lamw[:, 2:4], in_=lamw[:, 0:2], func=AF.Exp), reads=[b_lam], writes=[b_lam])
        p.op("dve", lambda e: e.tensor_tensor(out=lamw[:, 4:5], in0=lamw[:, 3:4], in1=lamw[:, 2:3], op=ALU.subtract),
             reads=[b_lam], writes=[b_lam])
        p.op("dve", lambda e: e.tensor_scalar(out=lamw[:, 4:5], in0=lamw[:, 4:5], scalar1=-0.2, scalar2=None, op0=ALU.add),
             reads=[b_lam], writes=[b_lam])
        p.op("dve", lambda e: e.tensor_scalar(out=sg08, in0=sg08, scalar1=0.8, scalar2=None, op0=ALU.mult),
             reads=[b_sg], writes=[b_sg])
        neglam = lamw[:, 4:5]

        if upto == 0:
            finish_early()
            return nc
        wq = []

        def wsrc(wt, r0, c0):
            return wt[r0:r0 + 2048, c0:c0 + 512].rearrange("(k p) n -> p k n", p=128)

        class WStream:
            def __init__(self):
                self.issued = 0
                self.got = 0
                self.hold = 10 ** 9

            def _issue(self):
                i = self.issued
                if i >= len(wq):
                    return
                s = i % 2
                src = wq[i]
                p.dma("pool", lambda e, s=s, src=src: e.dma_start(out=wviews[s], in_=src), wchans[s], writes=[wbufs[s]])
                self.issued += 1

            def get(self):
                i = self.got
                while self.issued <= min(i + 1, len(wq) - 1, self.hold):
                    self._issue()
                self.got += 1
                return wviews[i % 2], wbufs[i % 2]

        colbase = {"dq": 0, "dk": 1024, "dv": 2048, "nq": 3072, "nk": 4096, "nv": 5120}
        _G_ALL = [("dq", 0), ("dq", 1), ("dk", 0), ("dk", 1), ("dv", 0), ("dv", 1),
                  ("nq", 0), ("nq", 1), ("nk", 0), ("nk", 1), ("nv", 0), ("nv", 1)]
        _G_KV = [("dk", 0), ("dk", 1), ("dv", 0), ("dv", 1)]
        _G_HALO = [("nk", 0), ("nk", 1), ("nv", 0), ("nv", 1)]
        for lst in (_G_ALL, [("dk", 0), ("dk", 1)]):
            for (ty, g) in lst:
                wq.append(wsrc(w_in, 0, colbase[ty] + g * 512))
        N_A_W = len(wq)
        for cg in range(4):
            wq.append(wsrc(w_out, 0, cg * 512))
        for qd in range(4):
            for fg in range(4):
                wq.append(wsrc(w_up, 0, qd * 2048 + fg * 512))
            for cg in range(4):
                wq.append(wsrc(w_down, qd * 2048, cg * 512))
        W = WStream()
        W.hold = N_A_W - 1

        al = Alloc(BASE)
        xring = al.ring("xt", 3, [128, D], F32)
        hring = al.ring("h", 2, [128, D], BF16, chans=False)
        HTLO = al.lo
        hTb = []
        for i in range(2):
            v, _ = sb(al.lo, [128, 16, 1024], BF16)
            hTb.append(v)
            al.lo += 16 * 1024 * 2
        bhT = [[p.buf(f"hT{i}_{t}") for t in range(8)] for i in range(2)]
        hTh, _ = sb(al.lo, [128, 16, 512], BF16)
        al.lo += 16 * 512 * 2
        bhTh = [p.buf(f"hTh_{t}") for t in range(4)]
        halo_ready = [False] * 4
        cs_cv, cs_sv, b_csc, b_css = [], [], [], []
        for i in range(2):
            v, b = al.take(f"cs_c{i}", [128, 8, 16], F32)
            cs_cv.append(v)
            b_csc.append(b)
            v, b = al.take(f"cs_s{i}", [128, 8, 16], F32)
            cs_sv.append(v)
            b_css.append(b)
        c_cs = [p.chan("cs0"), p.chan("cs1")]
        sring = al.ring("stg", 4, [128, 512], BF16)
        srot_bufs = [p.buf(f"stgrot{i}") for i in range(4)]
        rtring_a = al.ring("rta", 2, [128, 8, 16], F32, chans=False)
        rtring_b = al.ring("rtb", 2, [128, 8, 16], F32, chans=False)
        tring = al.ring("stT", 3, [128, 4, 128], BF16)
        Rv, Rb = [], []
        for i in range(2):
            v, b = al.take(f"R{i}", [128, 16, 512], BF16)
            Rv.append(v)
            Rb.append(b)
        c_R = p.chan("R")
        assert al.lo <= ARENA_BYTES, al.lo
        resident = {}

        bKTd = [p.buf(f"KTd{h}") for h in range(8)]
        bVd = [p.buf(f"Vd{h}") for h in range(8)]
        bQTd = [p.buf(f"QTd{h}") for h in range(8)]
        bKTn = [p.buf(f"KTn{h}") for h in range(8)]
        bVn = [p.buf(f"Vn{h}") for h in range(8)]
        bQTn = [p.buf(f"QTn{h}") for h in range(8)]

        stat_i = [0]

        stat_bufs = [p.buf(f"stat{i}") for i in range(16)]

        def rms_tile(src_ap_fn, src_reads, gb_buf, out_view, out_buf, jk):
            si = stat_i[0] % 16
            stat_i[0] += 1
            st = stat[:, si * 4:si * 4 + 4]
            bst = stat_bufs[si]
            xa = src_ap_fn()
            jv, jb = jk
            p.op("act", lambda e: e.activation(out=jv, in_=xa, func=AF.Square, scale=1.0 / math.sqrt(D),
                                               accum_out=st[:, 0:1]),
                 reads=src_reads, writes=[jb, bst])
            p.op("act", lambda e: e.activation(out=st[:, 1:2], in_=st[:, 0:1], func=AF.Sqrt, bias=epsc, scale=1.0),
                 reads=[bst, b_eps], writes=[bst])
            p.op("dve", lambda e: e.reciprocal(out=st[:, 2:3], in_=st[:, 1:2]), reads=[bst], writes=[bst])
            p.op("dve", lambda e: e.scalar_tensor_tensor(out=out_view, in0=xa, scalar=st[:, 2:3], in1=gbc,
                                                         op0=ALU.mult, op1=ALU.mult),
                 reads=src_reads + [bst, gb_buf], writes=[out_buf])

        gemm_bank = [0]
        tp_bank = [0]

        def transposes_to(h_view, h_buf, dst, dst_bufs, col0, banks):
            for half in range(2):
                bj = banks[half]
                pb = bank_bf(bj)
                for k in range(8):
                    kk = half * 8 + k
                    p.op("pe", lambda e, kk=kk, k=k, pb=pb: e.transpose(pb[:, k * 128:(k + 1) * 128],
                                                                      h_view[:, kk * 128:(kk + 1) * 128], ident),
                         reads=[h_buf, b_ident], writes=[PS[bj]])
                eng = "act" if half == 0 else "dve"
                dv = dst[:, half * 8:(half + 1) * 8, col0:col0 + 128]
                src = pb.rearrange("p (a b) -> p a b", a=8)
                if eng == "act":
                    p.op("act", lambda e, dv=dv, src=src: e.copy(out=dv, in_=src), reads=[PS[bj]], writes=dst_bufs)
                else:
                    p.op("dve", lambda e, dv=dv, src=src: e.tensor_copy(out=dv, in_=src), reads=[PS[bj]], writes=dst_bufs)

        G_ALL = [("dq", 0), ("dq", 1), ("dk", 0), ("dk", 1), ("dv", 0), ("dv", 1),
                 ("nq", 0), ("nq", 1), ("nk", 0), ("nk", 1), ("nv", 0), ("nv", 1)]
        G_KV = [("dv", 0), ("dv", 1), ("dk", 0), ("dk", 1)]
        G_HALO = [("nk", 0), ("nk", 1), ("nv", 0), ("nv", 1)]
        subphases = [dict(kind="own", ntile=8, groups=G_ALL, tb=0),
                     dict(kind="rest", ntile=8, groups=G_KV, tb=1024),
                     dict(kind="rest", ntile=8, groups=G_KV, tb=2048),
                     dict(kind="rest", ntile=8, groups=G_KV, tb=3072)]

        c_csS = [p.chan("css0"), p.chan("css1")]
        h_inflight = {}

        def hT_stage1(si, t):
            if si >= 0:
                sp_ = subphases[si]
                par = si % 2
                tb = sp_["tb"]
                if t == 0:
                    p.dma("sp", lambda e, tb=tb, par=par: e.dma_start(
                        out=cs_cv[par], in_=cosx[tb:tb + 1024, :].rearrange("(t p) j -> p t j", p=128)),
                        c_cs[par], writes=[b_csc[par]])
                    p.dma("sp", lambda e, tb=tb, par=par: e.dma_start(
                        out=cs_sv[par], in_=sinx[tb:tb + 1024, :].rearrange("(t p) j -> p t j", p=128)),
                        c_csS[par], writes=[b_css[par]])
                src = xp[tb + t * 128: tb + (t + 1) * 128, :]
            else:
                src = xh[t * 128:(t + 1) * 128, :]
            xv, xb, xc = xring.next()
            p.dma("sp", lambda e, xv=xv, src=src: e.dma_start(out=xv, in_=src), xc, writes=[xb])
            hv, hb, _ = hring.next()
            rms_tile(lambda xv=xv: xv, [xb], b_g, hv, hb, (hv, hb))
            h_inflight[(si, t)] = (hv, hb)

        def hT_stage2(si, t):
            hv, hb = h_inflight.pop((si, t))
            if si >= 0:
                par = si % 2
                transposes_to(hv, hb, hTb[par], [bhT[par][t]], t * 128, (6, 7))
            else:
                transposes_to(hv, hb, hTh, [bhTh[t]], t * 128, (6, 7))
                halo_ready[t] = True

        def hT_tile_ops(si, t):
            hT_stage1(si, t)
            hT_stage2(si, t)

        last_pe = [None]
        for si, sp_ in enumerate(subphases):
            par = si % 2
            hT = hTb[par]
            ntile = sp_["ntile"]
            tb = sp_["tb"]
            if si == 0:
                for t in range(ntile):
                    hT_tile_ops(0, t)
            nxt_tiles = [(si + 1, t) for t in range(subphases[si + 1]["ntile"])] if si + 1 < len(subphases) else []
            if si == 0:
                nxt_tiles = [(-1, t) for t in range(4)] + nxt_tiles
            n_gt = len(sp_["groups"]) * ntile + (16 if si == 0 else 0)
            interval = max(1, n_gt // (len(nxt_tiles) + 3)) if nxt_tiles else 1
            stage2_q = []

            def insertion_point(flush=False):
                if len(stage2_q) >= 2 or (flush and stage2_q):
                    hT_stage2(*stage2_q.pop(0))
                if nxt_tiles:
                    tl_ = nxt_tiles.pop(0)
                    hT_stage1(*tl_)
                    stage2_q.append(tl_)

            gt = 0
            pending = []

            def pop_pending(force=False):
                while pending and (force or len(pending) > 2):
                    f_ = pending.pop(0)
                    if f_ is not None:
                        f_()

            if si == 0:
                for g_ in range(2):
                    src_ = wsrc(w_in, 0, colbase["dv"] + g_ * 512)
                    p.dma("pool", lambda e, g_=g_, src_=src_: e.dma_start(out=Rv[g_], in_=src_), c_R, writes=[Rb[g_]])
                    resident[("dv", g_)] = (Rv[g_], Rb[g_])
            for (ty, g) in sp_["groups"]:
                if sp_["kind"] == "rest":
                    if (ty, g) not in resident:
                        resident[(ty, g)] = W.get()
                    wv, wb = resident[(ty, g)]
                else:
                    wv, wb = W.get()
                tile_list = list(range(ntile))
                if sp_["kind"] == "own" and ty in ("nk", "nv"):
                    tile_list = tile_list + [8, 9, 10, 11]
                for t in tile_list:
                    bj = gemm_bank[0] % 4
                    gemm_bank[0] += 1
                    pbank = bank(bj)
                    if t >= 8:
                        assert halo_ready[t - 8]
                        hsrc, hbuf, tl = hTh, bhTh[t - 8], t - 8
                    else:
                        hsrc, hbuf, tl = hT, bhT[par][t], t
                    for k in range(16):
                        last_pe[0] = p.op("pe", lambda e, k=k, tl=tl, wv=wv, pbank=pbank, hsrc=hsrc: e.matmul(
                            pbank, lhsT=hsrc[:, k, tl * 128:(tl + 1) * 128], rhs=wv[:, k, :],
                            start=(k == 0), stop=(k == 15)),
                            reads=[hbuf, wb], writes=[PS[bj]])
                    if t >= 8:
                        dtok = (t - 8) * 128 if t < 10 else 1280 + (t - 10) * 128
                    elif ty in ("nk", "nv"):
                        dtok = 256 + 128 * t
                    else:
                        dtok = tb + t * 128
                    sv, sbf, sc = sring.next()
                    srb = srot_bufs[sring.i]
                    heads = [g * 4 + hh for hh in range(4)]
                    part2 = None
                    if ty in ("dv", "nv"):
                        p.op("act", lambda e, sv=sv, pbank=pbank: e.copy(out=sv, in_=pbank), reads=[PS[bj]], writes=[sbf, srb])
                        dst = (Vd if ty == "dv" else Vn)[dtok:dtok + 128, g * 512:(g + 1) * 512]
                        dbufs = [(bVd if ty == "dv" else bVn)[h] for h in heads]
                        p.dma("sp", lambda e, dst=dst, sv=sv: e.dma_start(out=dst, in_=sv), sc,
                              reads=[sbf, srb], writes=dbufs)
                    else:
                        if ty in ("dq", "dk"):
                            sv3 = sv.rearrange("p (g d) -> p g d", g=8)
                            pb3 = pbank.rearrange("p (g d) -> p g d", g=8)
                            p.op("act", lambda e, sv3=sv3, pb3=pb3: e.copy(out=sv3[:, :, 16:64], in_=pb3[:, :, 16:64]),
                                 reads=[PS[bj]], writes=[sbf])
                            ra, rab, _ = rtring_a.next()
                            rb, rbb, _ = rtring_b.next()
                            cc = cs_cv[par][:, t, :].unsqueeze(1).broadcast_to([128, 8, 16])
                            ss_ = cs_sv[par][:, t, :].unsqueeze(1).broadcast_to([128, 8, 16])
                            p.op("dve", lambda e, ra=ra, pb3=pb3, cc=cc: e.tensor_tensor(out=ra, in0=pb3[:, :, 0:16], in1=cc, op=ALU.mult),
                                 reads=[PS[bj], b_csc[par]], writes=[rab])
                            p.op("dve", lambda e, rb=rb, pb3=pb3, ss_=ss_: e.tensor_tensor(out=rb[:, :, 0:8], in0=pb3[:, :, 8:16], in1=ss_[:, :, 0:8], op=ALU.mult),
                                 reads=[PS[bj], b_css[par]], writes=[rbb])
                            p.op("dve", lambda e, rb=rb, pb3=pb3, ss_=ss_: e.tensor_tensor(out=rb[:, :, 8:16], in0=pb3[:, :, 0:8], in1=ss_[:, :, 8:16], op=ALU.mult),
                                 reads=[PS[bj], b_css[par], rbb], writes=[rbb])
                            p.op("dve", lambda e, sv3=sv3, ra=ra, rb=rb: e.tensor_tensor(out=sv3[:, :, 0:16], in0=ra, in1=rb, op=ALU.add),
                                 reads=[rab, rbb], writes=[srb])
                        else:
                            p.op("act", lambda e, sv=sv, pbank=pbank: e.copy(out=sv, in_=pbank), reads=[PS[bj]], writes=[sbf, srb])
                        dT = {"dq": QTd, "dk": KTd, "nq": QTn, "nk": KTn}[ty]
                        dB = {"dq": bQTd, "dk": bKTd, "nq": bQTn, "nk": bKTn}[ty]

                        def part2(sv=sv, sbf=sbf, srb=srb, sc=sc, dT=dT, dB=dB, heads=heads, dtok=dtok):
                            tj = 4 + tp_bank[0] % 2
                            tp_bank[0] += 1
                            pb = bank_bf(tj)
                            for hh in range(4):
                                last_pe[0] = p.op("pe", lambda e, hh=hh, pb=pb: e.transpose(pb[:, hh * 128:(hh + 1) * 128],
                                                                                           sv[:, hh * 128:(hh + 1) * 128], ident),
                                                  reads=[sbf, srb, b_ident], writes=[PS[tj]])
                            tv, tbf, tc = tring.next()
                            p.op("dve", lambda e, tv=tv, pb=pb: e.tensor_copy(out=tv, in_=pb[:, 0:512].rearrange("p (a b) -> p a b", a=4)),
                                 reads=[PS[tj]], writes=[tbf])
                            dst = dT[heads[0]:heads[0] + 4, :, dtok:dtok + 128].rearrange("h p t -> p h t")
                            p.dma("sp", lambda e, dst=dst, tv=tv: e.dma_start(out=dst, in_=tv), tc,
                                  reads=[tbf], writes=[dB[h] for h in heads])

                    pending.append(part2)
                    pop_pending()
                    gt += 1
                    if (nxt_tiles or stage2_q) and gt % interval == 0:
                        insertion_point()
            pop_pending(force=True)
            while nxt_tiles or stage2_q:
                insertion_point(flush=True)

        hT_region = sbuf_buf("hT_region", HTLO, HTLO + 2 * 16 * 1024 * 2 + 16 * 512 * 2)
        hT_region.r.append(last_pe[0])
        W.hold = 10 ** 9
        b_mix = [[sbuf_buf(f"mix{h}_{q}", MIXLO + h * 2048 + q * 1024, MIXLO + h * 2048 + (q + 1) * 1024)
                  for q in range(2)] for h in range(16)]

        if upto == 1:
            for ch in sring.chans + tring.chans:
                dbg_chans.append(ch)
            finish_early()
            return nc
        al = Alloc(BASE)
        qring = al.ring("QT", 2, [128, NOWN], BF16)
        kring = al.ring("KT", 2, [128, SEQ], BF16)
        vring = al.ring("V", 2, [128, 32, 128], BF16)
        pring = al.ring("P", 4, [128, 1024], BF16, chans=False)
        p2bufs = [p.buf(f"P2_{i}") for i in range(4)]
        accring = al.ring("acc", 2, [128, 512], F32, chans=False)
        ep = {}
        for nm in ("r1", "a", "r2", "b", "o", "sd", "rs", "lg0", "lg1", "lg2"):
            ep[nm] = al.take("ep_" + nm, [128, 512], F32)
        sqv, b_sq = al.take("sq", [128, 512], BF16)
        tblring = al.ring("tbl", 6, [128, 512], F32)
        assert al.lo <= MIXLO, al.lo

        def load_head_d(h):
            qv, qb, qc_ = qring.next()
            kv, kb_, kc = kring.next()
            vv, vb, vc = vring.next()
            p.dma("sp", lambda e: e.dma_start(out=qv, in_=QTd[h]), qc_, reads=[bQTd[h]], writes=[qb])
            p.dma("sp", lambda e: e.dma_start(out=kv, in_=KTd[h]), kc, reads=[bKTd[h]], writes=[kb_])
            for j4 in range(4):
                p.dma("sp", lambda e, j4=j4: e.dma_start(
                    out=vv[:, j4 * 8:(j4 + 1) * 8, :],
                    in_=Vd[j4 * 1024:(j4 + 1) * 1024, h * 128:(h + 1) * 128].rearrange("(kb p) d -> p kb d", p=128)),
                    vc, reads=[bVd[h]], writes=[vb])
            return (qv, qb), (kv, kb_), (vv, vb)

        SC_D = 1.0 / math.sqrt(64.0)
        deferred_b = []
        nxt = load_head_d(0)
        for h in range(8):
            (qv, qb), (kv, kb_), (vv, vb) = nxt
            if h + 1 < 8:
                nxt = load_head_d(h + 1)
            for qc in range(2):
                qs = slice(qc * 512, (qc + 1) * 512)
                accv, accb, _ = accring.next()

                def s_step(kb, qs=qs, qv=qv, qb=qb, kv=kv, kb_=kb_):
                    par = kb % 2
                    b1, b2 = 2 * par, 2 * par + 1
                    p.op("pe", lambda e: e.matmul(bank(b1), lhsT=kv[0:64, kb * 128:(kb + 1) * 128], rhs=qv[0:64, qs],
                                                  start=True, stop=True), reads=[kb_, qb], writes=[PS[b1]])
                    p.op("pe", lambda e: e.matmul(bank(b2), lhsT=kv[64:128, kb * 128:(kb + 1) * 128], rhs=qv[64:128, qs],
                                                  start=True, stop=True), reads=[kb_, qb], writes=[PS[b2]])
                    pv, pb_, _ = pring.next()
                    pb2_ = p2bufs[pring.i]
                    p.op("act", lambda e: e.activation(out=pv[:, 0:512], in_=bank(b1), func=AF.Exp, scale=SC_D),
                         reads=[PS[b1]], writes=[pb_])
                    p.op("act", lambda e: e.activation(out=pv[:, 512:1024], in_=bank(b2), func=AF.Exp, scale=SC_D),
                         reads=[PS[b2]], writes=[pb2_])
                    return (pv, pb_, pb2_)

                def o_step(kb, P, vv=vv, vb=vb, accv=accv, accb=accb):
                    st, sp__ = (kb == 0), (kb == 31)
                    p.op("pe", lambda e: e.matmul(bank(4), lhsT=vv[:, kb, :], rhs=P[0][:, 0:512], start=st, stop=sp__),
                         reads=[vb, P[1]], writes=[PS[4]])
                    p.op("pe", lambda e: e.matmul(bank(6), lhsT=ones, rhs=P[0][:, 0:512], start=st, stop=sp__),
                         reads=[b_ones, P[1]], writes=[PS[6]])
                    p.op("pe", lambda e: e.matmul(bank(5), lhsT=vv[:, kb, :], rhs=P[0][:, 512:1024], start=st, stop=sp__),
                         reads=[vb, P[2]], writes=[PS[5]])
                    if kb == 0:
                        p.op("dve", lambda e: e.tensor_copy(out=accv, in_=P[0][:, 512:1024]), reads=[P[2]], writes=[accb])
                    else:
                        p.op("dve", lambda e: e.tensor_tensor(out=accv, in0=accv, in1=P[0][:, 512:1024], op=ALU.add),
                             reads=[P[2], accb], writes=[accb])

                prev = s_step(0)
                for kb in range(32):
                    cur = prev
                    if kb + 1 < 32:
                        prev = s_step(kb + 1)
                    o_step(kb, cur)
                    if kb == 5 and deferred_b:
                        deferred_b.pop(0)()
                r1, a_, r2, b_, o_, sd, rs = (ep[n] for n in ("r1", "a", "r2", "b", "o", "sd", "rs"))
                p.op("pe", lambda e, accv=accv: e.matmul(bank(7), lhsT=onesf, rhs=accv, start=True, stop=True),
                     reads=[b_onesf, accb], writes=[PS[7]])
                t1, t2 = ep["lg0"], ep["lg1"]
                p.op("act", lambda e: e.activation(out=t1[0], in_=bank(6), func=AF.Ln), reads=[PS[6]], writes=[t1[1]])
                p.op("act", lambda e: e.activation(out=r1[0], in_=t1[0], func=AF.Exp, scale=-1.0), reads=[t1[1]], writes=[r1[1]])
                p.op("act", lambda e: e.activation(out=t2[0], in_=bank(7), func=AF.Ln), reads=[PS[7]], writes=[t2[1]])
                p.op("act", lambda e: e.activation(out=r2[0], in_=t2[0], func=AF.Exp, scale=-1.0), reads=[t2[1]], writes=[r2[1]])
                p.op("dve", lambda e: e.tensor_tensor(out=a_[0], in0=bank(4), in1=r1[0], op=ALU.mult),
                     reads=[PS[4], r1[1]], writes=[a_[1]])
                p.op("dve", lambda e: e.tensor_tensor(out=b_[0], in0=bank(5), in1=r2[0], op=ALU.mult),
                     reads=[PS[5], r2[1]], writes=[b_[1]])
                p.op("dve", lambda e: e.scalar_tensor_tensor(out=o_[0], in0=b_[0], scalar=neglam, in1=a_[0],
                                                             op0=ALU.mult, op1=ALU.add),
                     reads=[b_[1], a_[1], b_lam], writes=[o_[1]])
                p.op("dve", lambda e: e.tensor_tensor(out=sqv, in0=o_[0], in1=o_[0], op=ALU.mult), reads=[o_[1]], writes=[b_sq])

                def ep_part_b(h=h, qs=qs, qc=qc, o_=o_, sd=sd, rs=rs):
                    p.op("pe", lambda e: e.matmul(bank(7), lhsT=ones, rhs=sqv, start=True, stop=True),
                         reads=[b_ones, b_sq], writes=[PS[7]])
                    p.op("act", lambda e: e.activation(out=sd[0], in_=bank(7), func=AF.Ln, bias=epsc, scale=1.0 / 128.0),
                         reads=[PS[7], b_eps], writes=[sd[1]])
                    p.op("act", lambda e: e.activation(out=rs[0], in_=sd[0], func=AF.Exp, scale=-0.5), reads=[sd[1]], writes=[rs[1]])
                    p.op("dve", lambda e: e.scalar_tensor_tensor(out=mixT[:, h, qs], in0=o_[0], scalar=sg08, in1=rs[0],
                                                                 op0=ALU.mult, op1=ALU.mult),
                         reads=[o_[1], rs[1], b_sg], writes=[b_mix[h][qc]])

                deferred_b.append(ep_part_b)
        while deferred_b:
            deferred_b.pop(0)()

        if upto == 2:
            c_dbg = p.chan("dbg")
            dbg_chans.append(c_dbg)
            p.dma("sp", lambda e: e.dma_start(out=mixdbg, in_=mixT), c_dbg,
                  reads=[b_mix[h][q] for h in range(8) for q in range(2)], writes=[p.buf("mixdbg")])
            finish_early()
            return nc
        SC_N = 1.0 / math.sqrt(128.0)

        def load_head_n(h):
            qv, qb, qc_ = qring.next()
            kv, kb_, kc = kring.next()
            vv, vb, vc = vring.next()
            p.dma("sp", lambda e: e.dma_start(out=qv, in_=QTn[h]), qc_, reads=[bQTn[h]], writes=[qb])
            p.dma("sp", lambda e: e.dma_start(out=kv[:, 0:1536], in_=KTn[h]), kc, reads=[bKTn[h]], writes=[kb_])
            p.dma("sp", lambda e: e.dma_start(out=vv[:, 0:12, :], in_=Vn[:, h * 128:(h + 1) * 128].rearrange("(kb p) d -> p kb d", p=128)),
                  vc, reads=[bVn[h]], writes=[vb])
            return (qv, qb), (kv, kb_), (vv, vb)

        lgs = [ep["lg0"], ep["lg1"], ep["lg2"], ep["sd"]]
        lgi = [0]
        heads_n = {}
        heads_n[0] = load_head_n(0)
        steps = [(h, c, kb) for h in range(8) for c in range(2) for kb in range(8)]
        DEPTH = 3

        def n_s_step(i):
            h, c, kb = steps[i]
            (qv, qb), (kv, kb_), (vv, vb) = heads_n[h]
            qs = slice(c * 512, (c + 1) * 512)
            bj = i % 4
            k0 = c * 512 + kb * 128
            tv, tbf, tc = tblring.next()
            p.dma("sp", lambda e: e.dma_start(out=tv, in_=tbl[h, c, kb]), tc, writes=[tbf])
            p.op("pe", lambda e: e.matmul(bank(bj), lhsT=kv[:, k0:k0 + 128], rhs=qv[:, qs], start=True, stop=True),
                 reads=[kb_, qb], writes=[PS[bj]])
            lg = lgs[lgi[0] % 4]
            lgi[0] += 1
            p.op("dve", lambda e: e.scalar_tensor_tensor(out=lg[0], in0=bank(bj), scalar=SC_N, in1=tv,
                                                         op0=ALU.mult, op1=ALU.add),
                 reads=[PS[bj], tbf], writes=[lg[1]])
            p1f, p1b, _ = pring.next()
            p1 = p1f[:, 0:512]
            p.op("act", lambda e: e.activation(out=p1, in_=lg[0], func=AF.Exp), reads=[lg[1]], writes=[p1b])
            return (p1, p1b)

        def n_o_step(i, P1):
            h, c, kb = steps[i]
            (qv, qb), (kv, kb_), (vv, vb) = heads_n[h]
            qs = slice(c * 512, (c + 1) * 512)
            gpar = (h * 2 + c) % 2
            bo, bl = (4, 6) if gpar == 0 else (5, 7)
            if c == 0 and kb == 0 and h + 1 < 8:
                heads_n[h + 1] = load_head_n(h + 1)
            st, sp__ = (kb == 0), (kb == 7)
            p.op("pe", lambda e: e.matmul(bank(bo), lhsT=vv[:, c * 4 + kb, :], rhs=P1[0], start=st, stop=sp__),
                 reads=[vb, P1[1]], writes=[PS[bo]])
            if kb == 7:
                p.op("pe", lambda e: e.matmul(bank(bl), lhsT=zerosb, rhs=P1[0], start=False, stop=False),
                     reads=[b_zeros, P1[1]], writes=[PS[bl]])
            p.op("pe", lambda e: e.matmul(bank(bl), lhsT=ones, rhs=P1[0], start=st, stop=sp__),
                 reads=[b_ones, P1[1]], writes=[PS[bl]])
            if kb < 7:
                p.op("pe", lambda e: e.matmul(bank(bl), lhsT=zerosb, rhs=P1[0], start=False, stop=False),
                     reads=[b_zeros, P1[1]], writes=[PS[bl]])
            if kb == 7:
                t1 = ep["r1"] if gpar == 0 else ep["r2"]
                rr = ep["a"] if gpar == 0 else ep["b"]
                p.op("act", lambda e: e.activation(out=t1[0], in_=bank(bl), func=AF.Ln), reads=[PS[bl]], writes=[t1[1]])
                p.op("act", lambda e: e.activation(out=rr[0], in_=t1[0], func=AF.Exp, scale=-1.0), reads=[t1[1]], writes=[rr[1]])
                p.op("dve", lambda e: e.tensor_tensor(out=mixT[:, 8 + h, qs], in0=bank(bo), in1=rr[0], op=ALU.mult),
                     reads=[PS[bo], rr[1]], writes=[b_mix[8 + h][c]])

        sres = {}
        for i in range(len(steps) + DEPTH):
            if i < len(steps):
                h_i = steps[i][0]
                if h_i not in heads_n:
                    heads_n[h_i] = load_head_n(h_i)
                sres[i] = n_s_step(i)
            j = i - DEPTH
            if j >= 0:
                n_o_step(j, sres.pop(j))


        if debug:
            c_dbg = p.chan("dbg")
            p.dma("sp", lambda e: e.dma_start(out=mixdbg, in_=mixT), c_dbg,
                  reads=[b_mix[h][q] for h in range(16) for q in range(2)], writes=[p.buf("mixdbg")])

        if upto == 3:
            dbg_chans.append(c_dbg)
            finish_early()
            return nc
        al = Alloc(BASE)
        x1, _ = sb(al.lo, [128, 8, D], F32)
        X1LO = al.lo
        al.lo += 8 * D * 4
        b_x1 = [[sbuf_buf(f"x1_{t}_{cg}", X1LO + t * 8192 + cg * 2048, X1LO + t * 8192 + (cg + 1) * 2048)
                 for cg in range(4)] for t in range(8)]
        c_x1 = p.chan("x1")
        xnT, _ = sb(al.lo, [128, 16, NOWN], BF16)
        XNLO = al.lo
        al.lo += 32768
        hcring = al.ring("hc", 2, [128, D], BF16, chans=False)
        rtring = al.ring("rt", 3, [128, 512], F32, chans=False)
        junk2, b_junk2 = al.take("junk2", [128, D], BF16)
        assert al.lo <= MIXLO, al.lo

        for t in range(8):
            p.dma("sp", lambda e, t=t: e.dma_start(out=x1[:, t, :], in_=xp[t * 128:(t + 1) * 128, :]), c_x1,
                  writes=b_x1[t])
        gb = 0
        p.dma("sp", lambda e: e.dma_start(out=gbc, in_=g_mlp.partition_broadcast(128)), c_g, writes=[b_g])
        b_xnT = [p.buf(f"xnT{t}") for t in range(8)]
        xn_region = sbuf_buf("xnT_region", XNLO, XNLO + 32768)
        for t in range(8):
            b_xnT[t].r.extend(xn_region.r)
        c2_q = []

        def c2_stage1(t):
            hv, hb, _ = hcring.next()
            rms_tile(lambda t=t: x1[:, t, :], b_x1[t], b_g, hv, hb, (hv, hb))
            c2_q.append((t, hv, hb))

        def c2_stage2():
            t, hv, hb = c2_q.pop(0)
            banks = (4, 5) if t % 2 == 0 else (6, 7)
            transposes_to(hv, hb, xnT, [b_xnT[t]], t * 128, banks)

        for cg in range(4):
            wv, wb = W.get()
            for t in range(8):
                bj = gb % 4
                gb += 1
                for k in range(16):
                    p.op("pe", lambda e, k=k, t=t, wv=wv, bj=bj: e.matmul(bank(bj), lhsT=mixT[:, k, t * 128:(t + 1) * 128],
                                                                       rhs=wv[:, k, :], start=(k == 0), stop=(k == 15)),
                         reads=[b_mix[k][t // 4], wb], writes=[PS[bj]])
                xs = x1[:, t, cg * 512:(cg + 1) * 512]
                p.op("dve", lambda e, xs=xs, bj=bj: e.tensor_tensor(out=xs, in0=bank(bj), in1=xs, op=ALU.add),
                     reads=[PS[bj], b_x1[t][cg]], writes=[b_x1[t][cg]])
                if cg == 3:
                    if len(c2_q) >= 2:
                        c2_stage2()
                    c2_stage1(t)
        while c2_q:
            c2_stage2()

        aT = mixT
        b_aT = [[p.buf(f"aT{k}_{tc}") for tc in range(2)] for k in range(16)]
        mix_region = sbuf_buf("aT_region", MIXLO, MIXLO + 32768)
        for k in range(16):
            for tc in range(2):
                b_aT[k][tc].r.extend(mix_region.r)
        for qd in range(4):
            for fg in range(4):
                wv, wb = W.get()
                for fb in range(4):
                    kk = fg * 4 + fb
                    for tc in range(2):
                        bj = gb % 4
                        gb += 1
                        for k in range(16):
                            p.op("pe", lambda e, k=k, wv=wv, fb=fb, tc=tc, bj=bj: e.matmul(
                                bank(bj), lhsT=wv[:, k, fb * 128:(fb + 1) * 128], rhs=xnT[:, k, tc * 512:(tc + 1) * 512],
                                start=(k == 0), stop=(k == 15)),
                                reads=[wb] + [b_xnT[t] for t in range(tc * 4, tc * 4 + 4)], writes=[PS[bj]])
                        rv, rb_, _ = rtring.next()
                        p.op("act", lambda e, rv=rv, bj=bj: e.activation(out=rv, in_=bank(bj), func=AF.Relu),
                             reads=[PS[bj]], writes=[rb_])
                        p.op("dve", lambda e, rv=rv, kk=kk, tc=tc: e.tensor_tensor(out=aT[:, kk, tc * 512:(tc + 1) * 512],
                                                                                in0=rv, in1=rv, op=ALU.mult),
                             reads=[rb_], writes=[b_aT[kk][tc]])
            if qd == 3:
                p.dma("sp", lambda e: e.dma_start(out=gbc, in_=g_fin.partition_broadcast(128)), c_g, writes=[b_g])
                xn_all = [x for bb in b_xnT for x in ([bb.w] if bb.w is not None else []) + bb.r]
                alo = Alloc(XNLO)
                ostring = alo.ring("ost", 2, [128, D], F32)
                for b in ostring.bufs:
                    b.r.extend(xn_all)
                c_out = p.chan("out")
            for cg in range(4):
                wv, wb = W.get()
                for t in range(8):
                    bj = gb % 4
                    gb += 1
                    for k in range(16):
                        p.op("pe", lambda e, k=k, t=t, wv=wv, bj=bj: e.matmul(bank(bj), lhsT=aT[:, k, t * 128:(t + 1) * 128],
                                                                           rhs=wv[:, k, :], start=(k == 0), stop=(k == 15)),
                             reads=[b_aT[k][t // 4], wb], writes=[PS[bj]])
                    xs = x1[:, t, cg * 512:(cg + 1) * 512]
                    p.op("dve", lambda e, xs=xs, bj=bj: e.tensor_tensor(out=xs, in0=bank(bj), in1=xs, op=ALU.add),
                         reads=[PS[bj], b_x1[t][cg]], writes=[b_x1[t][cg]])
                    if qd == 3 and cg == 3:
                        ov, ob, _ = ostring.next()
                        rms_tile(lambda t=t: x1[:, t, :], b_x1[t], b_g, ov, ob, (junk2, b_junk2))
                        p.dma("sp", lambda e, ov=ov, t=t: e.dma_start(out=out[t * 128:(t + 1) * 128, :], in_=ov), c_out,
                              reads=[ob], writes=[p.buf("outd")])

        p.wait_chan("sp", c_out)
        if debug:
            p.wait_chan("sp", c_dbg)
        p.emit()
        p.close()
    return nc


def _rot_tables(order):
    inv = np.power(np.float32(500000.0), -np.arange(0, 16, 2, dtype=np.float32) / np.float32(16)).astype(np.float32)
    t = order.astype(np.float32)
    ang = (t[:, None] * inv[None, :]).astype(np.float32)
    c = np.cos(ang).astype(np.float32)
    s = np.sin(ang).astype(np.float32)
    c16 = np.concatenate([c, c], axis=1)
    s16 = np.concatenate([-s, s], axis=1)
    return np.ascontiguousarray(c16), np.ascontiguousarray(s16)


def _na_table(rel_bias, r):
    tblc = np.full((8, 2, 8, 128, 512), NEG, dtype=np.float32)
    k = np.arange(128)
    q = np.arange(512)
    for c in range(2):
        qrow = 16 * r + 8 * c + q // 64
        qcol = q % 64
        rs = np.clip(qrow - 4, 0, 56)
        cst = np.clip(qcol - 8, 0, 48)
        for kb in range(8):
            krow = 16 * r - 4 + 8 * c + 2 * kb + k // 64
            kcol = k % 64
            valid = ((krow[:, None] >= 0) & (krow[:, None] < 64)
                     & (krow[:, None] >= rs[None, :]) & (krow[:, None] < rs[None, :] + 8)
                     & (kcol[:, None] >= cst[None, :]) & (kcol[:, None] < cst[None, :] + 16))
            ri = np.clip(krow[:, None] - qrow[None, :] + 7, 0, 14)
            ci = np.clip(kcol[:, None] - qcol[None, :] + 15, 0, 30)
            g = rel_bias[:, ri, ci]
            tblc[:, c, kb] = np.where(valid[None], g, np.float32(NEG))
    return tblc


_NC_CACHE = {}


def kernel(x, norm_mix_g, w_in, lambda_q1, lambda_k1, lambda_q2, lambda_k2, diff_subln_g,
           na_rel_bias, w_out, norm_mlp_g, w_up, w_down, norm_final_g, _debug=False, _upto=9, _cores=None):
    x = np.asarray(x, dtype=np.float32)
    f = lambda a: np.ascontiguousarray(np.asarray(a, dtype=np.float32))
    w_in0, w_out0, w_up0, w_down0 = f(w_in[0]), f(w_out[0]), f(w_up[0]), f(w_down[0])
    rel = f(na_rel_bias[0])
    in_maps = []
    cores = list(range(8)) if _cores is None else list(_cores)
    for c in cores:
        b, r = c // 4, c % 4
        own = np.arange(r * 1024, (r + 1) * 1024)
        rest = np.concatenate([np.arange(0, r * 1024), np.arange((r + 1) * 1024, SEQ)])
        order = np.concatenate([own, rest])
        xpc = np.ascontiguousarray(x[b][order])
        xhc = np.zeros((512, D), dtype=np.float32)
        for i, row in enumerate(list(range(16 * r - 4, 16 * r)) + list(range(16 * r + 16, 16 * r + 20))):
            if 0 <= row < 64:
                xhc[i * 64:(i + 1) * 64] = x[b, row * 64:(row + 1) * 64]
        cx, sx = _rot_tables(order)
        in_maps.append({
            "xp": xpc, "xh": xhc, "cosx": cx, "sinx": sx, "tbl": _na_table(rel, r),
            "w_in": w_in0, "w_out": w_out0, "w_up": w_up0, "w_down": w_down0,
            "g_mix": f(norm_mix_g[0]), "g_mlp": f(norm_mlp_g[0]), "g_fin": f(norm_final_g),
            "lq1": f(lambda_q1[0]), "lk1": f(lambda_k1[0]), "lq2": f(lambda_q2[0]), "lk2": f(lambda_k2[0]),
            "subg": f(diff_subln_g[0]),
        })
    key = (bool(_debug), _upto)
    if key not in _NC_CACHE:
        _NC_CACHE[key] = build_program(debug=_debug, upto=_upto)
    nc = _NC_CACHE[key]
    res = run_bass_kernel_spmd(nc, in_maps, core_ids=list(range(len(cores))))
    outp = np.zeros((2, SEQ, D), dtype=np.float32)
    for i, c in enumerate(cores):
        b, r = c // 4, c % 4
        outp[b, r * 1024:(r + 1) * 1024] = res.results[i]["out"]
    if _debug:
        return outp, res
    return outp
```
